# Optimizing a Trainium2 kernel written in Bass

```python
import jax
import jax.numpy as jnp
from jax import lax
import numpy as np

D_MODEL = 1024
BATCH = 16
SEQ = 4096
DEPTH = 2
DEC_BATCH = 1
DEC_SEQ = 16384
PAST_LEN = 128

N_MIXERS = 4
Q_HEADS = 4
KV_HEADS = 2
GQA_GROUP = Q_HEADS // KV_HEADS
HEAD_DIM = D_MODEL // (N_MIXERS * Q_HEADS)
MIX_WIDTH = N_MIXERS * Q_HEADS * HEAD_DIM
KV_WIDTH = N_MIXERS * KV_HEADS * HEAD_DIM
IN_WIDTH = MIX_WIDTH + 2 * KV_WIDTH
D_FF = 2816
CONV_W = 3
GRID_W = 64
WIN_A = 128
BLOCK_A = 128
NA_ROWS = 8
NA_COLS = 16
DILATIONS = ((128, 1), (512, 4), (2048, 16))
BLOCK_C = 64
BLOCK_D = 128
ROPE_THETA = 10000.0
ROT_ROW = HEAD_DIM // 2
ROT_COL = HEAD_DIM - ROT_ROW
EPS = 1e-6
NEG_INF = -1e30

kernel_name = 'hybrid_parallel_head_encoder'


def rms_norm(x, w):
    xf = x.astype(jnp.float32)
    y = xf * lax.rsqrt(jnp.mean(xf * xf, axis=-1, keepdims=True) + EPS)
    return (y * w.astype(jnp.float32)).astype(x.dtype)


def alibi_slopes():
    n = 2 * Q_HEADS
    s = 2.0 ** (-8.0 * jnp.arange(1, n + 1, dtype=jnp.float32) / n)
    return s[0::2].reshape(KV_HEADS, GQA_GROUP), s[1::2].reshape(KV_HEADS, GQA_GROUP)


def alibi_bias(slopes, step):
    def bias(dist):
        return -slopes[:, :, None, None] * (step * dist).astype(jnp.float32)
    return bias


def banded_attention(q, k, v, radius, block, bias_fn, sink=None):
    n, length, hkv, g, dh = q.shape
    nblk = -(-length // block)
    lp = nblk * block
    wk = block + 2 * radius
    qp = jnp.pad(q, ((0, 0), (0, lp - length), (0, 0), (0, 0), (0, 0)))
    kv_pad = ((0, 0), (radius, lp - length + radius), (0, 0), (0, 0))
    kp = jnp.pad(k, kv_pad)
    vp = jnp.pad(v, kv_pad)
    scale = dh ** -0.5

    def one_block(i):
        start = i * block
        qb = lax.dynamic_slice_in_dim(qp, start, block, axis=1)
        kb = lax.dynamic_slice_in_dim(kp, start, wk, axis=1)
        vb = lax.dynamic_slice_in_dim(vp, start, wk, axis=1)
        qpos = start + jnp.arange(block)
        kpos = start - radius + jnp.arange(wk)
        dist = jnp.abs(qpos[:, None] - kpos[None, :])
        valid = (dist <= radius) & (kpos >= 0)[None, :] & (kpos < length)[None, :]
        s = jnp.einsum('nqhgd,nkhd->nhgqk', qb, kb, preferred_element_type=jnp.float32) * scale
        s = jnp.where(valid, s + bias_fn(dist), NEG_INF)
        m = jnp.max(s, axis=-1)
        if sink is not None:
            m = jnp.maximum(m, sink[:, :, None].astype(jnp.float32))
        p = jnp.exp(s - m[..., None])
        den = jnp.sum(p, axis=-1)
        if sink is not None:
            den = den + jnp.exp(sink[:, :, None].astype(jnp.float32) - m)
        o = jnp.einsum('nhgqk,nkhd->nqhgd', p.astype(vb.dtype), vb, preferred_element_type=jnp.float32)
        o = o / jnp.transpose(den, (0, 3, 1, 2))[..., None]
        lse = jnp.transpose(m + jnp.log(den), (0, 3, 1, 2))
        return o.astype(q.dtype), lse

    o, lse = lax.map(one_block, jnp.arange(nblk))
    o = jnp.moveaxis(o, 0, 1).reshape(n, lp, hkv, g, dh)[:, :length]
    lse = jnp.moveaxis(lse, 0, 1).reshape(n, lp, hkv, g)[:, :length]
    return o, lse


def neighbourhood_attention(q, k, v, rpb):
    b, s_len, hkv, g, dh = q.shape
    rows = s_len // GRID_W
    kr = min(NA_ROWS, rows)
    kc = NA_COLS
    qg = q.reshape(b, rows, GRID_W, hkv, g, dh)
    kg = k.reshape(b, rows, GRID_W, hkv, dh)
    vg = v.reshape(b, rows, GRID_W, hkv, dh)
    col = jnp.arange(GRID_W)
    col_idx = jnp.clip(col - kc // 2, 0, GRID_W - kc)[:, None] + jnp.arange(kc)[None, :]
    dc = col_idx - col[:, None] + (NA_COLS - 1)
    scale = dh ** -0.5

    def one_row(r):
        r0 = jnp.clip(r - kr // 2, 0, rows - kr)
        k_rows = lax.dynamic_slice_in_dim(kg, r0, kr, axis=1)
        v_rows = lax.dynamic_slice_in_dim(vg, r0, kr, axis=1)
        kn = k_rows[:, :, col_idx]
        vn = v_rows[:, :, col_idx]
        qr = lax.dynamic_index_in_dim(qg, r, axis=1, keepdims=False)
        s = jnp.einsum('bqhgd,brqchd->bhgqrc', qr, kn, preferred_element_type=jnp.float32) * scale
        dr = r0 + jnp.arange(kr) - r + (NA_ROWS - 1)
        s = s + rpb[:, :, dr[None, :, None], dc[:, None, :]].astype(jnp.float32)
        p = jax.nn.softmax(s.reshape(b, hkv, g, GRID_W, kr * kc), axis=-1).reshape(s.shape)
        o = jnp.einsum('bhgqrc,brqchd->bqhgd', p.astype(vn.dtype), vn, preferred_element_type=jnp.float32)
        return o.astype(q.dtype)

    o = lax.map(one_row, jnp.arange(rows))
    return jnp.moveaxis(o, 0, 1).reshape(b, s_len, hkv, g, dh)


def dilated_attention(q, k, v, slopes):
    b, s_len = q.shape[:2]
    outs, lses = [], []
    for window, dil in DILATIONS:
        sub = s_len // dil

        def to_strided(t):
            t = jnp.moveaxis(t.reshape((b, sub, dil) + t.shape[2:]), 2, 1)
            return t.reshape((b * dil, sub) + t.shape[3:])

        def from_strided(t):
            t = jnp.moveaxis(t.reshape((b, dil, sub) + t.shape[2:]), 1, 2)
            return t.reshape((b, s_len) + t.shape[3:])

        o, lse = banded_attention(to_strided(q), to_strided(k), to_strided(v),
                                  window // (2 * dil), BLOCK_C, alibi_bias(slopes, dil))
        outs.append(from_strided(o))
        lses.append(from_strided(lse))
    wts = jax.nn.softmax(jnp.stack(lses), axis=0)
    out = jnp.einsum('rbshg,rbshgd->bshgd', wts, jnp.stack(outs).astype(jnp.float32))
    return out.astype(q.dtype)


def axial_rope(x):
    s_len, dh = x.shape[1], x.shape[-1]
    t = jnp.arange(s_len)
    row = (t // GRID_W).astype(jnp.float32)
    col = (t % GRID_W).astype(jnp.float32)
    f_row = ROPE_THETA ** (-jnp.arange(0, ROT_ROW, 2, dtype=jnp.float32) / ROT_ROW)
    f_col = ROPE_THETA ** (-jnp.arange(0, ROT_COL, 2, dtype=jnp.float32) / ROT_COL)
    ang = jnp.concatenate([row[:, None] * f_row[None, :], col[:, None] * f_col[None, :]], axis=-1)
    shape = (s_len,) + (1,) * (x.ndim - 3) + (dh // 2,)
    cos = jnp.cos(ang).reshape(shape)
    sin = jnp.sin(ang).reshape(shape)
    xf = x.astype(jnp.float32).reshape(x.shape[:-1] + (dh // 2, 2))
    xe, xo = xf[..., 0], xf[..., 1]
    out = jnp.stack([xe * cos - xo * sin, xe * sin + xo * cos], axis=-1)
    return out.reshape(x.shape).astype(x.dtype)


def dense_attention(q, k, v):
    b, s_len, hkv, g, dh = q.shape
    nb = s_len // BLOCK_D
    qb = jnp.moveaxis(q.reshape(b, nb, BLOCK_D, hkv, g, dh), 1, 0)
    scale = dh ** -0.5

    def one_block(qi):
        s = jnp.einsum('bqhgd,bkhd->bhgqk', qi, k, preferred_element_type=jnp.float32) * scale
        p = jax.nn.softmax(s, axis=-1)
        o = jnp.einsum('bhgqk,bkhd->bqhgd', p.astype(v.dtype), v, preferred_element_type=jnp.float32)
        return o.astype(q.dtype)

    o = lax.map(one_block, qb)
    return jnp.moveaxis(o, 0, 1).reshape(q.shape)


def mixer_layer(x, norm_w, w_in, q_norm_w, k_norm_w, sink_a, rpb_b, out_norm_w, w_out):
    b, s_len, _ = x.shape
    h = rms_norm(x, norm_w)
    proj = jnp.einsum('bsd,de->bse', h, w_in)
    q = proj[..., :MIX_WIDTH].reshape(b, s_len, N_MIXERS, KV_HEADS, GQA_GROUP, HEAD_DIM)
    k = proj[..., MIX_WIDTH:MIX_WIDTH + KV_WIDTH].reshape(b, s_len, N_MIXERS, KV_HEADS, HEAD_DIM)
    v = proj[..., MIX_WIDTH + KV_WIDTH:].reshape(b, s_len, N_MIXERS, KV_HEADS, HEAD_DIM)
    q = rms_norm(q, q_norm_w[:, None, None, :])
    k = rms_norm(k, k_norm_w[:, None, :])
    slopes_a, slopes_c = alibi_slopes()
    o_a = banded_attention(q[:, :, 0], k[:, :, 0], v[:, :, 0], WIN_A, BLOCK_A,
                           alibi_bias(slopes_a, 1), sink_a.reshape(KV_HEADS, GQA_GROUP))[0]
    o_b = neighbourhood_attention(q[:, :, 1], k[:, :, 1], v[:, :, 1],
                                  rpb_b.reshape(KV_HEADS, GQA_GROUP, 2 * NA_ROWS - 1, 2 * NA_COLS - 1))
    o_c = dilated_attention(q[:, :, 2], k[:, :, 2], v[:, :, 2], slopes_c)
    o_d = dense_attention(axial_rope(q[:, :, 3]), axial_rope(k[:, :, 3]), v[:, :, 3])
    o = jnp.stack([o_a, o_b, o_c, o_d], axis=2).reshape(b, s_len, N_MIXERS, Q_HEADS * HEAD_DIM)
    o = rms_norm(o, out_norm_w.reshape(N_MIXERS, Q_HEADS * HEAD_DIM)).reshape(b, s_len, MIX_WIDTH)
    return x + jnp.einsum('bse,ed->bsd', o, w_out)


def conv_glu_ffn(x, norm_w, w_gate, w_val, conv_w, conv_b, w_down):
    h = rms_norm(x, norm_w)
    gate = jnp.einsum('bsd,df->bsf', h, w_gate)
    val = jnp.einsum('bsd,df->bsf', h, w_val)
    gate = lax.conv_general_dilated(gate, conv_w[:, None, :].astype(gate.dtype), window_strides=(1,),
                                    padding=((CONV_W // 2, CONV_W // 2),),
                                    dimension_numbers=('NWC', 'WIO', 'NWC'),
                                    feature_group_count=D_FF) + conv_b
    y = jax.nn.gelu(gate, approximate=False) * val
    return x + jnp.einsum('bsf,fd->bsd', y, w_down)


def trunk(x, norm1_w, w_in, q_norm_w, k_norm_w, sink_a, rpb_b, out_norm_w, w_out,
          norm2_w, w_gate, w_val, conv_w, conv_b, w_down):
    for l in range(DEPTH):
        x = mixer_layer(x, norm1_w[l], w_in[l], q_norm_w[l], k_norm_w[l], sink_a[l], rpb_b[l],
                        out_norm_w[l], w_out[l])
        x = conv_glu_ffn(x, norm2_w[l], w_gate[l], w_val[l], conv_w[l], conv_b[l], w_down[l])
    return x


def setup_inputs(seed: int = 0) -> dict:
    key = jax.random.key(seed)
    ks = jax.random.split(key, 16)
    nrm = jax.random.normal
    d, f = D_MODEL, D_FF
    return {
        'x_prompt': nrm(ks[0], (BATCH, SEQ, d), jnp.float32),
        'x_sample': nrm(ks[1], (DEC_BATCH, DEC_SEQ, d), jnp.float32),
        'norm1_w': 1.0 + 0.02 * nrm(ks[2], (DEPTH, d), jnp.float32),
        'w_in': nrm(ks[3], (DEPTH, d, IN_WIDTH), jnp.float32) * d ** -0.5,
        'q_norm_w': 1.0 + 0.02 * nrm(ks[4], (DEPTH, N_MIXERS, HEAD_DIM), jnp.float32),
        'k_norm_w': 1.0 + 0.02 * nrm(ks[5], (DEPTH, N_MIXERS, HEAD_DIM), jnp.float32),
        'sink_a': 0.5 * nrm(ks[6], (DEPTH, Q_HEADS), jnp.float32),
        'rpb_b': 0.1 * nrm(ks[7], (DEPTH, Q_HEADS, 2 * NA_ROWS - 1, 2 * NA_COLS - 1), jnp.float32),
        'out_norm_w': 1.0 + 0.02 * nrm(ks[8], (DEPTH, MIX_WIDTH), jnp.float32),
        'w_out': nrm(ks[9], (DEPTH, MIX_WIDTH, d), jnp.float32) * MIX_WIDTH ** -0.5,
        'norm2_w': 1.0 + 0.02 * nrm(ks[10], (DEPTH, d), jnp.float32),
        'w_gate': nrm(ks[11], (DEPTH, d, f), jnp.float32) * d ** -0.5,
        'w_val': nrm(ks[12], (DEPTH, d, f), jnp.float32) * d ** -0.5,
        'conv_w': nrm(ks[13], (DEPTH, CONV_W, f), jnp.float32) * CONV_W ** -0.5,
        'conv_b': 0.01 * nrm(ks[14], (DEPTH, f), jnp.float32),
        'w_down': nrm(ks[15], (DEPTH, f, d), jnp.float32) * f ** -0.5,
    }


def reference(x_prompt, x_sample, norm1_w, w_in, q_norm_w, k_norm_w, sink_a, rpb_b, out_norm_w,
              w_out, norm2_w, w_gate, w_val, conv_w, conv_b, w_down):
    y_prompt = trunk(x_prompt, norm1_w, w_in, q_norm_w, k_norm_w, sink_a, rpb_b, out_norm_w, w_out,
                     norm2_w, w_gate, w_val, conv_w, conv_b, w_down)
    y_sample = trunk(x_sample, norm1_w, w_in, q_norm_w, k_norm_w, sink_a, rpb_b, out_norm_w, w_out,
                     norm2_w, w_gate, w_val, conv_w, conv_b, w_down)
    return (y_prompt, y_sample)
```

```python
import contextlib
import math
import numpy as np
import ml_dtypes
import concourse.bass as bass
import concourse.mybir as mybir
from concourse.bass_utils import run_bass_kernel_spmd

F32 = mybir.dt.float32
BF16 = mybir.dt.bfloat16
AF = mybir.ActivationFunctionType
ALU = mybir.AluOpType
AX = mybir.AxisListType

D_MODEL = 1024
N_MIX = 4
HEAD_DIM = 64
IN_WIDTH = 2048
D_FF = 2816
NFC = D_FF // 128
GRID_W = 64
EPS = 1e-6
NEG = -30000.0
VW = 80
N_CORES = 8
SEM_ROLL = 20000


class K:
    def __init__(self, nc, es):
        self.nc = nc
        self.es = es
        self.eng = {"pe": nc.tensor, "act": nc.scalar, "dve": nc.vector, "pool": nc.gpsimd, "sp": nc.sync}
        self.waited = {e: {} for e in self.eng}
        self.streams = {}
        self.nsem = 0
        self.n_instr = 0
        self.scopes = [es]
        self.slots = {}

    def new_sem(self, name):
        self.nsem += 1
        return self.es.enter_context(self.nc.semaphore(f"{name}_{self.nsem}"))

    def sb(self, name, shape, dt):
        self.uid = getattr(self, "uid", 0) + 1
        return self.scopes[-1].enter_context(self.nc.sbuf_tensor(f"{name}_{self.uid}", list(shape), dt))

    def ps(self, name, shape, dt):
        self.uid = getattr(self, "uid", 0) + 1
        return self.scopes[-1].enter_context(self.nc.psum_tensor(f"{name}_{self.uid}", list(shape), dt))

    @contextlib.contextmanager
    def scope(self):
        st = contextlib.ExitStack()
        self.scopes.append(st)
        try:
            yield
            self.barrier()
        finally:
            self.scopes.pop()
            st.close()

    def all_tokens(self):
        toks = []
        for e, st in self.streams.items():
            toks.append((st[2], st[0], st[1]))
        for sl in self.slots.values():
            if sl.cnt:
                toks.append(sl.tok())
        return toks

    def barrier(self):
        toks = self.all_tokens()
        for e in self.eng:
            self.wait(e, *toks)

    def sig(self, e, instr):
        st = self.streams.get(e)
        if st is None or st[1] >= SEM_ROLL:
            st = [self.new_sem("s" + e), 0, self.nsem]
            self.streams[e] = st
        instr.then_inc(st[0], 1)
        st[1] += 1
        self.n_instr += 1
        return (st[2], st[0], st[1])

    def wait(self, e, *toks):
        w = self.waited[e]
        for t in toks:
            if t is None:
                continue
            key, sem, val = t
            if w.get(key, 0) >= val:
                continue
            self.eng[e].wait_ge(sem, val)
            w[key] = val

    def dma_slot(self, name):
        if name not in self.slots:
            self.slots[name] = DmaSlot(self, name)
        return self.slots[name]


class DmaSlot:
    def __init__(self, k, name):
        self.k = k
        self.sem = k.new_sem("d" + name)
        self.key = k.nsem
        self.cnt = 0

    def dma(self, e, out, in_):
        self.k.eng[e].dma_start(out=out, in_=in_).then_inc(self.sem, 16)
        self.cnt += 16
        self.k.n_instr += 1
        return (self.key, self.sem, self.cnt)

    def tok(self):
        return (self.key, self.sem, self.cnt)


class Ring:
    STRICT = True
    VIOLATIONS = []

    def __init__(self, bufs, name="ring"):
        self.bufs = bufs
        self.n = len(bufs)
        self.k = 0
        self.readers = [[] for _ in bufs]
        self.open = [False] * self.n
        self.name = name

    def next(self):
        i = self.k % self.n
        self.k += 1
        if Ring.STRICT and self.open[i]:
            Ring.VIOLATIONS.append((self.name, i, self.k - 1))
        self.open[i] = True
        r = self.readers[i]
        self.readers[i] = []
        return i, self.bufs[i], r

    def done(self, i, *toks):
        self.open[i] = False
        self.readers[i].extend(t for t in toks if t is not None)


def alibi_slopes_np():
    n = 8
    s = 2.0 ** (-8.0 * np.arange(1, n + 1, dtype=np.float64) / n)
    return s[0::2].reshape(2, 2), s[1::2].reshape(2, 2)


def bias_tiles_A():
    sa, _ = alibi_slopes_np()
    j = np.arange(128)[:, None]
    i = np.arange(128)[None, :]
    out = np.zeros((3, 128, 2, 2, 128), np.float32)
    for r, rel in enumerate((-1, 0, 1)):
        d = rel * 128 + j - i
        for h in range(2):
            for g in range(2):
                out[r, :, h, g, :] = np.where(np.abs(d) <= 128, -sa[h, g] * np.abs(d), NEG)
    return out


def bias_tiles_C():
    _, sc = alibi_slopes_np()
    j = np.arange(128)[:, None]
    i = np.arange(128)[None, :]
    out = np.zeros((17, 128, 2, 2, 128), np.float32)
    for r, rel in enumerate(range(-8, 9)):
        d = rel * 128 + j - i
        ad = np.abs(d)
        mult = (ad <= 64).astype(np.float64) + ((d % 4 == 0) & (ad <= 256)) + ((d % 16 == 0) & (ad <= 1024))
        for h in range(2):
            for g in range(2):
                val = -sc[h, g] * ad + np.log(np.maximum(mult, 1.0))
                out[r, :, h, g, :] = np.where(mult > 0, val, NEG)
    return out


def nbr_plan(L):
    rows = L // GRID_W
    kr = min(8, rows)
    nb = L // 128
    plan = []
    for b in range(nb):
        r0s = [int(np.clip(r - kr // 2, 0, rows - kr)) for r in (2 * b, 2 * b + 1)]
        chunks = sorted({rr // 2 for r0 in r0s for rr in range(r0, r0 + kr)})
        plan.append([(c, (r0s[0] - 2 * b, r0s[1] - (2 * b + 1), c - b)) for c in chunks])
    return plan


def bias_tile_B(key, rpb_idx):
    o0, o1, relc = key
    col = np.arange(GRID_W)
    c0 = np.clip(col - 8, 0, GRID_W - 16)
    idx = np.full((128, 128), 15 * 31, np.int64)
    for qi in range(128):
        qr_rel, qc = divmod(qi, GRID_W)
        r0_rel = (o0, o1 + 1)[qr_rel]
        for kj in range(128):
            kr_rel2, kc = divmod(kj, GRID_W)
            krow_rel = 2 * relc + kr_rel2
            if not (r0_rel <= krow_rel < r0_rel + 8):
                continue
            if not (c0[qc] <= kc < c0[qc] + 16):
                continue
            dr = krow_rel - qr_rel + 7
            dc = kc - qc + 15
            idx[kj, qi] = dr * 31 + dc
    return idx


def rope_table(L):
    t = np.arange(L)
    row = (t // GRID_W).astype(np.float32)
    col = (t % GRID_W).astype(np.float32)
    f = (10000.0 ** (-np.arange(0, 32, 2, dtype=np.float32) / 32)).astype(np.float32)
    ang = np.concatenate([row[:, None] * f[None, :], col[:, None] * f[None, :]], axis=-1).astype(np.float32)
    return np.concatenate([np.cos(ang), np.sin(ang)], axis=-1).astype(np.float32)


class Prog:
    def __init__(self, seqs, depth=2, dbg=None):
        self.seqs = seqs
        self.depth = depth
        self.dbg = dbg or {}
        self.T = sum(L for _, L in seqs)
        self.plan_B()
        self.nBt = len(self.keysB)
        self.off = {}
        o = 0
        for n, L in seqs:
            self.off[n] = o
            o += L

    def build(self):
        nc = bass.Bass("TRN2", target_bir_lowering=False)
        self.nc = nc
        T = self.T
        dep = self.depth
        din = lambda n, s, d=F32: nc.dram_tensor(n, list(s), d, kind="ExternalInput").ap()
        self.x_in = din("x", [T, D_MODEL])
        self.norm1 = din("norm1_w", [dep, D_MODEL])
        self.w_in = din("w_in", [dep, D_MODEL, IN_WIDTH])
        self.qkg = din("qk_gain", [dep, 1536])
        self.sink = din("sink_a", [dep, 4])
        self.onw = din("out_norm_w", [dep, D_MODEL])
        self.w_out = din("w_out", [dep, D_MODEL, D_MODEL])
        self.norm2 = din("norm2_w", [dep, D_MODEL])
        self.w_gate = din("w_gate", [dep, D_MODEL, D_FF])
        self.w_val = din("w_val", [dep, D_MODEL, D_FF])
        self.conv_w = din("conv_w", [dep, 3, D_FF])
        self.conv_b = din("conv_b", [dep, D_FF])
        self.w_down = din("w_down", [dep, D_FF, D_MODEL])
        self.rope = din("rope", [max(L for _, L in self.seqs), 64])
        self.biasA = din("biasA", [3, 128, 2, 256], BF16)
        self.biasC = din("biasC", [17, 128, 2, 256], BF16)
        self.biasB = din("biasB", [dep, self.nBt, 128, 2, 256])
        self.ident_in = din("ident", [128, 128])
        self.y_out = nc.dram_tensor("y", [T, D_MODEL], F32, kind="ExternalOutput").ap()
        dint = lambda n, s, d: nc.dram_tensor(n, list(s), d).ap()
        self.qT_d = dint("qT_d", [8, 128, T], BF16)
        self.kT_d = dint("kT_d", [4, 128, T], BF16)
        self.v_d = dint("v_d", [T, N_MIX * 2 * VW], BF16)
        self.x1_d = dint("x1_d", [T, D_MODEL], F32)
        self.xl_d = dint("xl_d", [T, D_MODEL], F32)
        self.h2T_d = dint("h2T_d", [8, 128, T + 2 * len(self.seqs)], BF16)
        self.o_d = dint("o_d", [T, D_MODEL], BF16)
        if "o" in self.dbg:
            self.dbg_o = nc.dram_tensor("dbg_o", [T, D_MODEL], BF16, kind="ExternalOutput").ap()
        if "qkv" in self.dbg:
            self.dbg_q = nc.dram_tensor("dbg_q", [8, 128, T], BF16, kind="ExternalOutput").ap()
            self.dbg_k = nc.dram_tensor("dbg_k", [4, 128, T], BF16, kind="ExternalOutput").ap()
            self.dbg_v = nc.dram_tensor("dbg_v", [T, N_MIX * 2 * VW], BF16, kind="ExternalOutput").ap()

        with contextlib.ExitStack() as es:
            k = K(nc, es)
            self.k = k
            self.setup_consts()
            final = []
            for l in range(dep):
                x_src = self.x_in if l == 0 else self.xl_d
                with k.scope():
                    self.phase_P(l, x_src)
                if self.dbg.get("stop") == "P":
                    break
                with k.scope():
                    self.phase_M(l)
                if self.dbg.get("stop") == "M":
                    break
                with k.scope():
                    self.phase_O(l, x_src)
                if self.dbg.get("stop") == "O":
                    break
                with k.scope():
                    self.phase_F(l, self.y_out if l == dep - 1 else self.xl_d)
            k.barrier()
        return nc

    def setup_consts(self):
        k, nc = self.k, self.nc
        self.ident_f = k.sb("ident_f", [128, 128], F32)
        self.ident_b = k.sb("ident_b", [128, 128], BF16)
        sl = k.dma_slot("const")
        t = sl.dma("sp", self.ident_f[:], self.ident_in[:, :])
        k.wait("dve", t)
        self.tok_ident = k.sig("dve", nc.vector.tensor_copy(out=self.ident_b[:], in_=self.ident_f[:]))
        self.const_slot = sl
        self.eps_t = k.sb("eps_t", [128, 1], F32)
        self.tok_eps = k.sig("dve", nc.vector.memset(self.eps_t[:], EPS))

    def phase_P(self, l, x_src):
        k, nc = self.k, self.nc
        T = self.T
        NT = T // 128
        sl = k.dma_slot("Pconst")
        w = k.sb(f"P_w{l}", [128, 8, IN_WIDTH], BF16)
        g1 = k.sb(f"P_g1{l}", [128, D_MODEL], F32)
        gqk = k.sb(f"P_gqk{l}", [128, 1536], F32)
        wslot = k.dma_slot("Pw")
        for c in range(8):
            for hh in range(2):
                wslot.dma("pool", w[:, c, hh * 1024:(hh + 1) * 1024], self.w_in[l, c * 128:(c + 1) * 128, hh * 1024:(hh + 1) * 1024])
        t_w = wslot.tok()
        sl.dma("sp", g1[:], self.norm1[l:l + 1, :].broadcast_to([128, D_MODEL]))
        t_g = sl.dma("sp", gqk[:], self.qkg[l:l + 1, :].broadcast_to([128, 1536]))
        k.wait("dve", t_g)
        t_g = k.sig("dve", nc.vector.tensor_scalar(out=gqk[:, 0:1024], in0=gqk[:, 0:1024], scalar1=0.125, scalar2=None, op0=ALU.mult))
        NX = 4
        xr = Ring([k.sb(f"P_x{l}_{i}", [128, D_MODEL], F32) for i in range(NX)])
        xslots = [k.dma_slot(f"Px{i}") for i in range(NX)]
        ropeb = Ring([k.sb(f"P_rope{l}_{i}", [128, 64], F32) for i in range(NX)])
        junk_ln = k.sb(f"P_junk{l}", [128, 1024], F32)
        sqr = Ring([k.sb(f"P_sq{l}_{i}", [128, 1536], F32) for i in range(2)])
        st_ring = Ring([k.sb(f"P_st{l}_{i}", [128, 4], F32) for i in range(2)])
        hb = Ring([k.sb(f"P_hb{l}_{i}", [128, D_MODEL], BF16) for i in range(2)])
        hT = Ring([k.sb(f"P_hT{l}_{i}", [128, 8, 128], BF16) for i in range(2)])
        psT = Ring([k.ps(f"P_psT{l}", [128, 8, 128], BF16)])
        psP = Ring([k.ps(f"P_psP{l}", [128, IN_WIDTH], F32)])
        psQ = Ring([k.ps(f"P_psQ{l}", [128, 12, 128], BF16)])
        s24 = Ring([k.sb(f"P_s24{l}_{i}", [128, 48], F32) for i in range(2)])
        tmp = Ring([k.sb(f"P_tmp{l}_{i}", [128, 1536], F32) for i in range(2)])
        rt = Ring([k.sb(f"P_rt{l}_{i}", [128, 6, 4, 32], F32) for i in range(2)])
        qn = Ring([k.sb(f"P_qn{l}_{i}", [128, 1536], BF16) for i in range(2)])
        NG = 2
        qst = Ring([k.sb(f"P_qst{l}_{i}", [128, 12, 512], BF16) for i in range(NG)])
        vst = Ring([k.sb(f"P_vst{l}_{i}", [128, 4, N_MIX * 2 * VW], BF16) for i in range(NG)])
        oslots = [k.dma_slot(f"Po{i}") for i in range(NG)]
        ones_toks = []
        for b in vst.bufs:
            ones_toks.append(k.sig("pool", nc.gpsimd.memset(b[:], 1.0)))
        seq_of_tile = []
        for n, L in self.seqs:
            for i in range(L // 128):
                seq_of_tile.append(i)

        def load(i):
            j, xb, rd = xr.next()
            k.wait("sp", *rd)
            t1 = xslots[j].dma("sp", xb[:], x_src[i * 128:(i + 1) * 128, :])
            _, rb, rd2 = ropeb.next()
            k.wait("sp", *rd2)
            p = seq_of_tile[i] * 128
            t2 = xslots[j].dma("sp", rb[:], self.rope[p:p + 128, :])
            return (j, xb, rb, t2)

        loaded = {}
        PRE = 2
        for i in range(min(PRE, NT)):
            loaded[i] = load(i)
        out_toks = []
        gi = None
        tails = []
        def front(i):
            if i + PRE < NT:
                loaded[i + PRE] = load(i + PRE)
            xj, xb, rb, t_x = loaded.pop(i)
            si, stt, rd = st_ring.next()
            k.wait("dve", t_x, *rd)
            k.wait("act", self.tok_eps)
            t = k.sig("dve", nc.vector.scalar_tensor_tensor(out=junk_ln[:], in0=xb[:], scalar=1.0, in1=xb[:],
                                                          op0=ALU.mult, op1=ALU.mult, accum_out=stt[:, 0:1]))
            k.wait("act", t)
            t = k.sig("act", nc.scalar.activation(out=stt[:, 1:2], in_=stt[:, 0:1], func=AF.Ln, bias=self.eps_t[:, 0:1], scale=1.0 / D_MODEL))
            k.wait("act", t)
            t = k.sig("act", nc.scalar.activation(out=stt[:, 2:3], in_=stt[:, 1:2], func=AF.Exp, scale=-0.5))
            hi, hbb, rd = hb.next()
            k.wait("dve", t, t_g, *rd)
            t_h = k.sig("dve", nc.vector.scalar_tensor_tensor(out=hbb[:], in0=xb[:], scalar=stt[:, 2:3], in1=g1[:],
                                                            op0=ALU.mult, op1=ALU.mult))
            xr.done(xj, t_h)
            st_ring.done(si, t_h)
            pi, pst, rd = psT.next()
            k.wait("pe", t_h, self.tok_ident, *rd)
            for c in range(8):
                ins = nc.tensor.transpose(pst[:, c, :], hbb[:, c * 128:(c + 1) * 128], self.ident_b[:])
            t_T = k.sig("pe", ins)
            hb.done(hi, t_T)
            ti, hTb, rd = hT.next()
            k.wait("act", t_T, *rd)
            t_hT = k.sig("act", nc.scalar.activation(out=hTb[:], in_=pst[:], func=AF.Copy))
            psT.done(pi, t_hT)
            return xj, rb, ti, hTb, t_hT

        def back(i, fr):
            nonlocal gi, qsb, vsb, grp_toks
            xj, rb, ti, hTb, t_hT = fr
            ppi, pp, rd = psP.next()
            k.wait("pe", t_hT, t_w, *rd)
            for jj in range(4):
                for c in range(8):
                    ins = nc.tensor.matmul(pp[:, jj * 512:(jj + 1) * 512], hTb[:, c, :], w[:, c, jj * 512:(jj + 1) * 512],
                                           start=(c == 0), stop=(c == 7))
            t_P = k.sig("pe", ins)
            hT.done(ti, t_P)
            while tails:
                tails.pop(0)()
            return ppi, pp, t_P

        def back_rest(i, fr, pr):
            nonlocal gi, qsb, vsb, grp_toks
            xj, rb, ti, hTb, t_hT = fr
            ppi, pp, t_P = pr
            sq_i, sqb, rd = sqr.next()
            k.wait("act", t_P, *rd)
            t_sq = k.sig("act", nc.scalar.activation(out=sqb[:], in_=pp[:, 0:1536], func=AF.Square))
            s_i, s24b, rd = s24.next()
            k.wait("dve", t_sq, *rd)
            t = k.sig("dve", nc.vector.tensor_reduce(out=s24b[:, 0:24], in_=sqb[:].rearrange("p (g d) -> p g d", d=64), axis=AX.X, op=ALU.add))
            sqr.done(sq_i, t)
            k.wait("dve", t)
            k.wait("act", t)
            t = k.sig("act", nc.scalar.activation(out=s24b[:, 24:48], in_=s24b[:, 0:24], func=AF.Ln, bias=self.eps_t[:, 0:1], scale=1.0 / 64))
            k.wait("act", t)
            t = k.sig("act", nc.scalar.activation(out=s24b[:, 0:24], in_=s24b[:, 24:48], func=AF.Exp, scale=-0.5))
            t_i, tmpb, rd = tmp.next()
            k.wait("dve", t, *rd)
            t_n1 = k.sig("dve", nc.vector.tensor_tensor(out=tmpb[:].rearrange("p (g d) -> p g d", d=64),
                                                      in0=pp[:, 0:1536].rearrange("p (g d) -> p g d", d=64),
                                                      in1=s24b[:, 0:24].unsqueeze(2).broadcast_to([128, 24, 64]), op=ALU.mult))
            s24.done(s_i, t_n1)
            k.wait("dve", t_n1, t_g)
            t_n2 = k.sig("dve", nc.vector.tensor_tensor(out=tmpb[:], in0=tmpb[:], in1=gqk[:], op=ALU.mult))
            q_i, qnb, rd = qn.next()
            k.wait("pool", *rd)
            k.wait("dve", *rd)
            tq = None
            for m in range(3):
                src = tmpb[:, m * 256:(m + 1) * 256].rearrange("p (h g d) -> p g h d", h=2, g=2)
                dst = qnb[:, m * 256:(m + 1) * 256].rearrange("p (g h d) -> p g h d", g=2, h=2)
                k.wait("pool", t_n2)
                tq = k.sig("pool", nc.gpsimd.tensor_copy(out=dst, in_=src))
            tk = k.sig("pool", nc.gpsimd.tensor_copy(out=qnb[:, 1024:1024 + 384], in_=tmpb[:, 1024:1024 + 384]))
            r_i, rtb, rd = rt.next()
            k.wait("dve", t_n2, *rd)
            k.wait("pool", t_n2, *rd)
            cosb = rb[:, 0:32].unsqueeze(1).broadcast_to([128, 4, 32])
            sinb = rb[:, 32:64].unsqueeze(1).broadcast_to([128, 4, 32])
            qv = tmpb[:, 768:1024].rearrange("p (a d two) -> p a d two", a=4, two=2)
            kv = tmpb[:, 1408:1536].rearrange("p (a d two) -> p a d two", a=2, two=2)
            qdst = qnb[:, 768:1024].rearrange("p (g h d two) -> p h g d two", g=2, h=2, two=2)
            kdst = qnb[:, 1408:1536].rearrange("p (a d two) -> p a d two", a=2, two=2)
            lasts = {}
            for (src, na, dsts, en, E) in ((qv, 4, "q", "dve", nc.vector), (kv, 2, "k", "pool", nc.gpsimd)):
                cb = cosb[:, 0:na, :]
                sb_ = sinb[:, 0:na, :]
                xe, xo = src[:, :, :, 0], src[:, :, :, 1]
                b0 = 0 if dsts == "q" else 3
                A = rtb[:, b0 + 0, 0:na, :]
                B = rtb[:, b0 + 1, 0:na, :]
                C = rtb[:, b0 + 2, 0:na, :]
                t1 = k.sig(en, E.tensor_tensor(out=A, in0=xe, in1=cb, op=ALU.mult))
                t2 = k.sig(en, E.tensor_tensor(out=B, in0=xo, in1=sb_, op=ALU.mult))
                k.wait(en, t1, t2)
                if dsts == "q":
                    for h in range(2):
                        t3 = k.sig(en, E.tensor_tensor(out=qdst[:, h, :, :, 0], in0=A[:, 2 * h:2 * h + 2, :], in1=B[:, 2 * h:2 * h + 2, :], op=ALU.subtract))
                else:
                    t3 = k.sig(en, E.tensor_tensor(out=kdst[:, :, :, 0], in0=A, in1=B, op=ALU.subtract))
                k.wait(en, t3)
                t4 = k.sig(en, E.tensor_tensor(out=A, in0=xe, in1=sb_, op=ALU.mult))
                t5 = k.sig(en, E.tensor_tensor(out=C, in0=xo, in1=cb, op=ALU.mult))
                k.wait(en, t4, t5)
                if dsts == "q":
                    for h in range(2):
                        lasts[dsts] = k.sig(en, E.tensor_tensor(out=qdst[:, h, :, :, 1], in0=A[:, 2 * h:2 * h + 2, :], in1=C[:, 2 * h:2 * h + 2, :], op=ALU.add))
                else:
                    lasts[dsts] = k.sig(en, E.tensor_tensor(out=kdst[:, :, :, 1], in0=A, in1=C, op=ALU.add))
            t_rope = lasts["q"]
            t_ropek = lasts["k"]
            rt.done(r_i, t_rope, t_ropek)
            ropeb.done(xj, t_rope, t_ropek)
            tmp.done(t_i, t_rope, t_ropek, tk, tq)
            sub = i % 4
            if sub == 0:
                gi, qsb, rdq = qst.next()
                _, vsb, rdv = vst.next()
                k.wait("act", *rdq, *rdv, *ones_toks)
                grp_toks = []
            vdst = vsb[:, sub, :].rearrange("p (a c) -> p a c", c=VW)[:, :, 0:64]
            k.wait("act", t_P)
            t_v = k.sig("act", nc.scalar.activation(out=vdst, in_=pp[:, 1536:2048].rearrange("p (a c) -> p a c", c=64), func=AF.Copy))
            psP.done(ppi, t_v, t_n1, t_sq)

            def tail(i=i, sub=sub, gi=gi, qsb=qsb, vsb=vsb, grp_toks=grp_toks, t_rope=t_rope, t_ropek=t_ropek, tq=tq, tk=tk, q_i=q_i, qnb=qnb, t_v=t_v):
                qi_, pq, rd = psQ.next()
                k.wait("pe", t_rope, t_ropek, tq, tk, *rd)
                for a_ in range(12):
                    ins = nc.tensor.transpose(pq[:, a_, :], qnb[:, a_ * 128:(a_ + 1) * 128], self.ident_b[:])
                t_qT = k.sig("pe", ins)
                qn.done(q_i, t_qT)
                k.wait("act", t_qT)
                t_qs = k.sig("act", nc.scalar.activation(out=qsb[:, :, sub * 128:(sub + 1) * 128], in_=pq[:], func=AF.Copy))
                psQ.done(qi_, t_qs)
                grp_toks += [t_qs, t_v]
                if sub == 3 or i == NT - 1:
                    t0 = (i // 4) * 512
                    n = (sub + 1) * 128
                    k.wait("sp", *grp_toks)
                    os_ = oslots[gi]
                    os_.dma("sp", self.qT_d[:, :, t0:t0 + n].rearrange("a p t -> p a t"), qsb[:, 0:8, 0:n])
                    os_.dma("sp", self.kT_d[:, :, t0:t0 + n].rearrange("a p t -> p a t"), qsb[:, 8:12, 0:n])
                    tt = os_.dma("sp", self.v_d[t0:t0 + n, :].rearrange("(s p) c -> p s c", p=128), vsb[:, 0:sub + 1, :])
                    if "qkv" in self.dbg and l == self.dbg["qkv"]:
                        os_.dma("sp", self.dbg_q[:, :, t0:t0 + n].rearrange("a p t -> p a t"), qsb[:, 0:8, 0:n])
                        os_.dma("sp", self.dbg_k[:, :, t0:t0 + n].rearrange("a p t -> p a t"), qsb[:, 8:12, 0:n])
                        tt = os_.dma("sp", self.dbg_v[t0:t0 + n, :].rearrange("(s p) c -> p s c", p=128), vsb[:, 0:sub + 1, :])
                    qst.done(gi, tt)
                    vst.done(gi, tt)
                    out_toks.append(tt)
            tails.append(tail)

        qsb = vsb = grp_toks = None
        fr = front(0)
        for i in range(NT):
            pr = back(i, fr)
            fr_next = front(i + 1) if i + 1 < NT else None
            back_rest(i, fr, pr)
            fr = fr_next
        for tl in tails:
            tl()
        return out_toks[-NG:]

    def plan_B(self):
        if hasattr(self, "_planB"):
            return
        self._planB = {}
        keys = {}
        for n, L in self.seqs:
            pl = nbr_plan(L)
            self._planB[n] = pl
            for blk in pl:
                for c, key in blk:
                    if key not in keys:
                        keys[key] = len(keys)
        self.keysB = keys

    def chunk_list(self, m, b, name, L, bA, bB, bC, h):
        nb = L // 128
        if m == 3:
            return [(c, None) for c in range(nb)]
        if m == 0:
            return [(c, bA[:, c - b + 1, h, :]) for c in (b - 1, b, b + 1) if 0 <= c < nb]
        if m == 1:
            return [(c, bB[:, self.keysB[key], h, :]) for c, key in self._planB[name][b]]
        return [(c, bC[:, c - b + 8, h, :]) for c in range(b - 8, b + 9) if 0 <= c < nb]

    def phase_M(self, l):
        k, nc = self.k, self.nc
        self.plan_B()
        nBt = len(self.keysB)
        assert nBt == self.nBt, (nBt, self.nBt)
        HALO = (128, 512, 1024, 0)
        bA = k.sb("M_bA", [128, 3, 2, 256], BF16)
        bC = k.sb("M_bC", [128, 17, 2, 256], BF16)
        bB = k.sb("M_bB", [128, nBt, 2, 256], BF16)
        snk = k.sb("M_snk", [128, 4], F32)
        cs = k.dma_slot("Mconst")
        cs.dma("sp", bA[:], self.biasA.rearrange("r p h c -> p r h c"))
        cs.dma("sp", bC[:], self.biasC.rearrange("r p h c -> p r h c"))
        for r in range(nBt):
            cs.dma("pool", bB[:, r, :, :], self.biasB[l, r])
        t_c = cs.dma("sp", snk[:], self.sink[l:l + 1, :].broadcast_to([128, 4]))
        k.wait("act", t_c)
        t_snk = k.sig("act", nc.scalar.activation(out=snk[:], in_=snk[:], func=AF.Exp))
        steps = []
        for n, L in self.seqs:
            QS = L if L <= 4096 else 2048
            for q_lo in range(0, L, QS):
                for m in range(4):
                    if m == 3:
                        lo, hi = 0, L
                    else:
                        lo, hi = max(0, q_lo - HALO[m]), min(L, q_lo + QS + HALO[m])
                    steps.append((n, L, q_lo, QS, m, lo, hi))
        kmax = [0, 0]
        qmax = 0
        for (n, L, q_lo, QS, m, lo, hi) in steps:
            si = 1 if m in (0, 2) else 0
            kmax[si] = max(kmax[si], hi - lo)
            qmax = max(qmax, QS)
        sets = []
        for si in range(2):
            sets.append(dict(kT=k.sb(f"M_kT{si}", [128, kmax[si]], BF16), v=k.sb(f"M_v{si}", [128, kmax[si] // 128, 2 * VW], BF16),
                             q=k.sb(f"M_q{si}", [128, 2, qmax], BF16), slot=k.dma_slot(f"Mset{si}"), readers=[]))
        Sring = Ring([k.ps(f"M_S{i}", [128, 1024], F32) for i in range(3)], "M.Sring")
        Aring = Ring([k.ps(f"M_acc{i}", [128, 512], F32) for i in range(2)], "M.Aring")
        Pring = Ring([k.sb(f"M_P{i}", [128, 1024], BF16) for i in range(3)], "M.Pring")
        Oring = Ring([k.sb(f"M_osb{i}", [80, 2, 512], F32) for i in range(2)], "M.Oring")
        rec_r = Ring([k.sb(f"M_rec{i}", [128, 4], F32) for i in range(2)])
        NOS = 4
        ost = Ring([k.sb(f"M_ost{i}", [128, 256], BF16) for i in range(NOS)])
        oslots = [k.dma_slot(f"Mo{i}") for i in range(NOS)]

        def load(step):
            n, L, q_lo, QS, m, lo, hi = step
            st = sets[1 if m in (0, 2) else 0]
            k.wait("sp", *st["readers"])
            st["readers"] = []
            sl = st["slot"]
            so = self.off[n]
            nk = hi - lo
            sl.dma("sp", st["kT"][:, 0:nk], self.kT_d[m, :, so + lo:so + hi])
            for c0 in range(0, nk // 128, 8):
                c1 = min(nk // 128, c0 + 8)
                sl.dma("sp", st["v"][:, c0:c1, :],
                       self.v_d[so + lo + c0 * 128:so + lo + c1 * 128, m * 2 * VW:(m + 1) * 2 * VW].rearrange("(c p) x -> p c x", p=128))
            for g in range(2):
                sl.dma("sp", st["q"][:, g, 0:QS], self.qT_d[m * 2 + g, :, so + q_lo:so + q_lo + QS])
            return st, sl.tok()

        nxt = load(steps[0])
        for sidx, step in enumerate(steps):
            st, t_ld = nxt
            if sidx + 1 < len(steps):
                nxt = load(steps[sidx + 1])
            n, L, q_lo, QS, m, lo, hi = step
            so = self.off[n]
            NT = 256 if (m == 3 and QS >= 256) else 128
            Nq = 2 * NT
            items = []
            for t0 in range(q_lo, q_lo + QS, NT):
                cls = [self.chunk_list(m, t0 // 128, n, L, bA, bB, bC, h) for h in range(2)]
                assert len(cls[0]) == len(cls[1])
                for ci in range(len(cls[0])):
                    for h in range(2):
                        c, bias = cls[h][ci]
                        items.append((t0, h, c, bias, ci == 0, ci == len(cls[0]) - 1))
            nit = len(items)
            G = 1024 // Nq
            assert nit % 2 == 0

            def colof(pos):
                return (pos % 2) * 512 + (pos // 2) * (Nq if Nq == 256 else 0)
            batches = [list(range(i0, min(nit, i0 + G))) for i0 in range(0, nit, G)]
            nbt = len(batches)
            LAB = 2
            binfo = {}
            pending = []
            cur_acc = {}
            cur_osb = {}
            last_pe = None

            def emit_S(bi):
                ensure_slot_free()
                si_, sps, rd = Sring.next()
                k.wait("pe", t_ld, *rd)
                ins = None
                bt = batches[bi]
                for j0 in range(0, len(bt), 2):
                    for pos in (j0, j0 + 1):
                        t0, h, c, bias, first, last = items[bt[pos]]
                        hp = slice(h * 64, (h + 1) * 64)
                        kc = (c * 128 - lo)
                        cols = slice(colof(pos), colof(pos) + Nq)
                        ins = nc.tensor.matmul(sps[:, cols], st["kT"][hp, kc:kc + 128], st["q"][hp, :, t0 - q_lo:t0 - q_lo + NT],
                                               start=True, stop=(bias is None))
                    for pos in (j0, j0 + 1):
                        bias = items[bt[pos]][3]
                        if bias is not None:
                            k.wait("pe", t_c, self.tok_ident)
                            cols = slice(colof(pos), colof(pos) + Nq)
                            ins = nc.tensor.matmul(sps[:, cols], self.ident_b[:], bias, start=False, stop=True)
                binfo[bi] = (si_, sps, k.sig("pe", ins))

            def epilogue_pe(t0, osb_i, osb, t_ev):
                nonlocal last_pe
                for blk in range(NT // 128):
                    ensure_slot_free()
                    ei, epsb, rd = Sring.next()
                    eps_ = epsb[:, 0:512].rearrange("p (a c) -> p a c", c=128)
                    k.wait("pe", *t_ev, *rd)
                    for h in range(2):
                        for g in range(2):
                            ins = nc.tensor.transpose(eps_[:, h * 2 + g, 0:VW], osb[0:VW, h, g * NT + blk * 128:g * NT + blk * 128 + 128],
                                                      self.ident_f[0:VW, 0:VW])
                    t_tr = k.sig("pe", ins)
                    last_pe = t_tr
                    if blk == NT // 128 - 1:
                        Oring.done(osb_i, t_tr)
                    ri, rec, rd = rec_r.next()
                    k.wait("dve", t_tr, *rd)
                    if m == 0:
                        k.wait("dve", t_snk)
                        t = k.sig("dve", nc.vector.tensor_tensor(out=rec[:], in0=eps_[:, :, 64], in1=snk[:], op=ALU.add))
                        k.wait("dve", t)
                        t = k.sig("dve", nc.vector.reciprocal(out=rec[:], in_=rec[:]))
                    else:
                        t = k.sig("dve", nc.vector.reciprocal(out=rec[:], in_=eps_[:, :, 64]))
                    oi, ob, rd = ost.next()
                    k.wait("dve", t, *rd)
                    t_o = k.sig("dve", nc.vector.tensor_tensor(out=ob[:].rearrange("p (a d) -> p a d", d=64), in0=eps_[:, :, 0:64],
                                                             in1=rec[:].unsqueeze(2).broadcast_to([128, 4, 64]), op=ALU.mult))
                    Sring.done(ei, t_o)
                    rec_r.done(ri, t_o)
                    k.wait("sp", t_o)
                    tok0 = so + t0 + blk * 128
                    t_d = oslots[oi].dma("sp", self.o_d[tok0:tok0 + 128, m * 256:(m + 1) * 256], ob[:])
                    ost.done(oi, t_d)

            def emit_EP(bj):
                nonlocal last_pe
                si_, sps, t_S = binfo.pop(bj)
                pi, pb, rd = Pring.next()
                nb_ = len(batches[bj])
                k.wait("act", t_S, *rd)
                wcol = 512 if Nq == 512 else (nb_ // 2) * 256
                t_E = k.sig("act", nc.scalar.activation(out=pb[:, :].rearrange("p (h c) -> p h c", h=2)[:, :, 0:wcol],
                                                        in_=sps[:, :].rearrange("p (h c) -> p h c", h=2)[:, :, 0:wcol], func=AF.Exp))
                Sring.done(si_, t_E)
                for pos, idx in enumerate(batches[bj]):
                    t0, h, c, bias, first, last = items[idx]
                    if first:
                        ai, acc, rd = Aring.next()
                        cur_acc[h] = (ai, acc)
                        k.wait("pe", *rd)
                    ai, acc = cur_acc[h]
                    k.wait("pe", t_E)
                    cl_ = c - lo // 128
                    t_PV = k.sig("pe", nc.tensor.matmul(acc[0:VW, 0:Nq], st["v"][:, cl_, h * VW:(h + 1) * VW], pb[:, colof(pos):colof(pos) + Nq],
                                                        start=first, stop=last))
                    last_pe = t_PV
                    if last:
                        if h == 0:
                            oi_, osb, rd = Oring.next()
                            cur_osb["o"] = (oi_, osb, [])
                            k.wait("dve", *rd)
                        oi_, osb, evs = cur_osb["o"]
                        k.wait("dve", t_PV)
                        t_ev = k.sig("dve", nc.vector.tensor_copy(out=osb[0:VW, h, 0:Nq], in_=acc[0:VW, 0:Nq]))
                        Aring.done(ai, t_ev)
                        evs.append(t_ev)
                        if h == 1:
                            pending.append((bj + 1, (t0, oi_, osb, list(evs))))
                Pring.done(pi, last_pe)

            ep_state = {"next": 0, "in_epi": False}

            def drain_pending():
                if ep_state["in_epi"]:
                    return
                ep_state["in_epi"] = True
                while pending:
                    _, args = pending.pop(0)
                    epilogue_pe(*args)
                ep_state["in_epi"] = False

            def run_ep_until(b):
                while ep_state["next"] <= b:
                    emit_EP(ep_state["next"])
                    ep_state["next"] += 1
                    drain_pending()

            def ensure_slot_free():
                while Sring.open[Sring.k % Sring.n]:
                    assert ep_state["next"] in binfo, "score ring slot still open but no emitted batch left to consume"
                    emit_EP(ep_state["next"])
                    ep_state["next"] += 1

            for bidx in range(nbt):
                emit_S(bidx)
                drain_pending()
                run_ep_until(bidx - LAB)
            run_ep_until(nbt - 1)
            drain_pending()
            st["readers"] = [last_pe]
        if "o" in self.dbg and l == self.dbg["o"]:
            k.barrier()
            t = k.dma_slot("dbgo").dma("sp", self.dbg_o[:, :], self.o_d[:, :])
            k.wait("sp", t)

    def phase_O(self, l, x_src):
        k, nc = self.k, self.nc
        T = self.T
        NT = T // 128
        cs = k.dma_slot("Oconst")
        w = k.sb("O_w", [128, 8, D_MODEL], BF16)
        gon = k.sb("O_gon", [128, D_MODEL], F32)
        g2 = k.sb("O_g2", [128, D_MODEL], F32)
        wslot = k.dma_slot("Ow")
        for c in range(8):
            wslot.dma("pool", w[:, c, :], self.w_out[l, c * 128:(c + 1) * 128, :])
        t_w = wslot.tok()
        cs.dma("sp", gon[:], self.onw[l:l + 1, :].broadcast_to([128, D_MODEL]))
        t_g = cs.dma("sp", g2[:], self.norm2[l:l + 1, :].broadcast_to([128, D_MODEL]))
        NX = 4
        xr = Ring([k.sb(f"O_x{i}", [128, D_MODEL], F32) for i in range(NX)])
        orr = Ring([k.sb(f"O_o{i}", [128, D_MODEL], BF16) for i in range(NX)])
        xslots = [k.dma_slot(f"Ox{i}") for i in range(NX)]
        sq = Ring([k.sb(f"O_sq{i}", [128, D_MODEL], F32) for i in range(2)])
        stt_r = Ring([k.sb(f"O_st{i}", [128, 16], F32) for i in range(2)])
        tmp = Ring([k.sb(f"O_tmp{i}", [128, D_MODEL], F32) for i in range(2)])
        onb = Ring([k.sb(f"O_on{i}", [128, D_MODEL], BF16) for i in range(2)])
        onT = Ring([k.sb(f"O_onT{i}", [128, 8, 128], BF16) for i in range(2)])
        psT = Ring([k.ps(f"O_psT{i}", [128, 8, 128], BF16) for i in range(2)])
        psY = Ring([k.ps(f"O_psY{i}", [128, D_MODEL], F32) for i in range(2)])
        x1r = Ring([k.sb(f"O_x1{i}", [128, D_MODEL], F32) for i in range(3)])
        x1slots = [k.dma_slot(f"Ox1{i}") for i in range(3)]
        junk = k.sb("O_junk", [128, D_MODEL], F32)
        hb = Ring([k.sb(f"O_hb{i}", [128, D_MODEL], BF16) for i in range(2)])
        NG = 2
        hst = Ring([k.sb(f"O_hst{i}", [128, 8, 512], BF16) for i in range(NG)])
        hslots = [k.dma_slot(f"Oh{i}") for i in range(NG)]
        tile_seq = []
        for si, (n, L) in enumerate(self.seqs):
            for i in range(L // 128):
                tile_seq.append((si, n, i))

        def load(i):
            j, xb, rd = xr.next()
            _, ob, rd2 = orr.next()
            k.wait("sp", *rd, *rd2)
            xslots[j].dma("sp", xb[:], x_src[i * 128:(i + 1) * 128, :])
            t = xslots[j].dma("sp", ob[:], self.o_d[i * 128:(i + 1) * 128, :])
            return j, xb, ob, t

        PRE = 2
        loaded = {i: load(i) for i in range(min(PRE, NT))}
        grp = None
        tails = []
        def front(i):
            if i + PRE < NT:
                loaded[i + PRE] = load(i + PRE)
            xj, xb, ob, t_x = loaded.pop(i)
            sqi, sqb, rd = sq.next()
            k.wait("act", t_x, *rd)
            t_sq = k.sig("act", nc.scalar.activation(out=sqb[:], in_=ob[:], func=AF.Square))
            si_, stt, rd = stt_r.next()
            k.wait("dve", t_sq, *rd)
            t = k.sig("dve", nc.vector.tensor_reduce(out=stt[:, 0:4], in_=sqb[:].rearrange("p (m d) -> p m d", d=256), axis=AX.X, op=ALU.add))
            sq.done(sqi, t)
            k.wait("act", t, self.tok_eps)
            t = k.sig("act", nc.scalar.activation(out=stt[:, 4:8], in_=stt[:, 0:4], func=AF.Ln, bias=self.eps_t[:, 0:1], scale=1.0 / 256))
            k.wait("act", t)
            t = k.sig("act", nc.scalar.activation(out=stt[:, 8:12], in_=stt[:, 4:8], func=AF.Exp, scale=-0.5))
            ti, tmpb, rd = tmp.next()
            k.wait("dve", t, *rd)
            t_n1 = k.sig("dve", nc.vector.tensor_tensor(out=tmpb[:].rearrange("p (m d) -> p m d", d=256), in0=ob[:].rearrange("p (m d) -> p m d", d=256),
                                                      in1=stt[:, 8:12].unsqueeze(2).broadcast_to([128, 4, 256]), op=ALU.mult))
            oi_, onbb, rd = onb.next()
            k.wait("dve", t_n1, t_g, *rd)
            t_n2 = k.sig("dve", nc.vector.tensor_tensor(out=onbb[:], in0=tmpb[:], in1=gon[:], op=ALU.mult))
            tmp.done(ti, t_n2)
            pi, pst, rd = psT.next()
            k.wait("pe", t_n2, self.tok_ident, *rd)
            for c in range(8):
                ins = nc.tensor.transpose(pst[:, c, :], onbb[:, c * 128:(c + 1) * 128], self.ident_b[:])
            t_T = k.sig("pe", ins)
            onb.done(oi_, t_T)
            tti, onTb, rd = onT.next()
            k.wait("act", t_T, *rd)
            t_c = k.sig("act", nc.scalar.activation(out=onTb[:], in_=pst[:], func=AF.Copy))
            psT.done(pi, t_c)
            return xj, xb, tti, onTb, t_c, si_, stt, t_n1, t_sq

        def back(i, fr):
            nonlocal grp
            xj, xb, tti, onTb, t_c, si_, stt, t_n1, t_sq = fr
            yi, py, rd = psY.next()
            k.wait("pe", t_c, t_w, *rd)
            for hf in range(2):
                for c in range(8):
                    ins = nc.tensor.matmul(py[:, hf * 512:(hf + 1) * 512], onTb[:, c, :], w[:, c, hf * 512:(hf + 1) * 512], start=(c == 0), stop=(c == 7))
            t_Y = k.sig("pe", ins)
            onT.done(tti, t_Y)
            while tails:
                tails.pop(0)()
            return yi, py, t_Y

        def back_rest(i, fr, pr):
            nonlocal grp
            xj, xb, tti, onTb, t_c, si_, stt, t_n1, t_sq = fr
            yi, py, t_Y = pr
            x1i, x1b, rd = x1r.next()
            k.wait("dve", t_Y, *rd)
            t_x1 = k.sig("dve", nc.vector.tensor_tensor(out=x1b[:], in0=py[:], in1=xb[:], op=ALU.add))
            psY.done(yi, t_x1)
            k.wait("sp", t_x1)
            t_st = x1slots[x1i].dma("sp", self.x1_d[i * 128:(i + 1) * 128, :], x1b[:])
            k.wait("dve", t_x1)
            t = k.sig("dve", nc.vector.scalar_tensor_tensor(out=junk[:], in0=x1b[:], scalar=1.0, in1=x1b[:], op0=ALU.mult, op1=ALU.mult,
                                                          accum_out=stt[:, 12:13]))
            k.wait("act", t)
            t = k.sig("act", nc.scalar.activation(out=stt[:, 13:14], in_=stt[:, 12:13], func=AF.Ln, bias=self.eps_t[:, 0:1], scale=1.0 / D_MODEL))
            k.wait("act", t)
            t = k.sig("act", nc.scalar.activation(out=stt[:, 14:15], in_=stt[:, 13:14], func=AF.Exp, scale=-0.5))
            hi, hbb, rd = hb.next()
            k.wait("dve", t, t_g, *rd)
            t_h = k.sig("dve", nc.vector.scalar_tensor_tensor(out=hbb[:], in0=x1b[:], scalar=stt[:, 14:15], in1=g2[:], op0=ALU.mult, op1=ALU.mult))
            stt_r.done(si_, t_h)
            x1r.done(x1i, t_h, t_st)
            xr.done(xj, t_x1)
            orr.done(xj, t_n1, t_sq)
            sqn, n, ti_in_seq = tile_seq[i]
            sub = ti_in_seq % 4
            if sub == 0:
                gi, hsb, rd = hst.next()
                k.wait("act", *rd)
                grp = (gi, hsb, i)
            gi, hsb, i0 = grp

            def tail(t_h=t_h, hbb=hbb, hi=hi, sqn=sqn, n=n, ti_in_seq=ti_in_seq, sub=sub, gi=gi, hsb=hsb):
                pi, pst, rd = psT.next()
                k.wait("pe", t_h, *rd)
                for c in range(8):
                    ins = nc.tensor.transpose(pst[:, c, :], hbb[:, c * 128:(c + 1) * 128], self.ident_b[:])
                t_T2 = k.sig("pe", ins)
                hb.done(hi, t_T2)
                k.wait("act", t_T2)
                t_hs = k.sig("act", nc.scalar.activation(out=hsb[:, :, sub * 128:(sub + 1) * 128], in_=pst[:], func=AF.Copy))
                psT.done(pi, t_hs)
                L = dict(self.seqs)[n]
                if sub == 3 or ti_in_seq == L // 128 - 1:
                    ncol = (sub + 1) * 128
                    col0 = self.off[n] + 2 * sqn + 1 + (ti_in_seq - sub) * 128
                    k.wait("sp", t_hs)
                    t_d = hslots[gi].dma("sp", self.h2T_d[:, :, col0:col0 + ncol].rearrange("a p t -> p a t"), hsb[:, :, 0:ncol])
                    hst.done(gi, t_d)
            tails.append(tail)

        fr = front(0)
        for i in range(NT):
            pr = back(i, fr)
            fr_next = front(i + 1) if i + 1 < NT else None
            back_rest(i, fr, pr)
            fr = fr_next
        for tl in tails:
            tl()

    def phase_F(self, l, dst):
        k, nc = self.k, self.nc
        wg = k.sb("F_wg", [128, 8, D_FF], BF16)
        wv = k.sb("F_wv", [128, 8, D_FF], BF16)
        wd = k.sb("F_wd", [128, NFC, D_MODEL], BF16)
        cw = k.sb("F_cw", [128, 3, NFC], F32)
        cb = k.sb("F_cb", [128, NFC], F32)
        wslot = k.dma_slot("Fw")
        for c in range(8):
            for (dstw, src) in ((wg, self.w_gate), (wv, self.w_val)):
                for f0 in range(0, D_FF, 1408):
                    wslot.dma("pool", dstw[:, c, f0:f0 + 1408], src[l, c * 128:(c + 1) * 128, f0:f0 + 1408])
        for fc in range(NFC):
            wslot.dma("pool", wd[:, fc, :], self.w_down[l, fc * 128:(fc + 1) * 128, :])
        t_w = wslot.tok()
        cs = k.dma_slot("Fconst")
        with nc.allow_non_contiguous_dma(reason="tiny per-feature conv params"):
            for j in range(3):
                cs.dma("sp", cw[:, j, :], self.conv_w[l, j, :].rearrange("(c p) -> p c", p=128))
            t_c = cs.dma("sp", cb[:], self.conv_b[l, :].rearrange("(c p) -> p c", p=128))
        hT = Ring([k.sb(f"F_hT{i}", [128, 8, 514], BF16) for i in range(2)])
        hslots = [k.dma_slot(f"Fh{i}") for i in range(2)]
        yT = k.sb("F_yT", [128, NFC, 512], BF16)
        yT_readers = [[] for _ in range(NFC)]
        cbuf = Ring([k.sb(f"F_c{i}", [128, 512], F32) for i in range(2)])
        glb = Ring([k.sb(f"F_g{i}", [128, 512], F32) for i in range(2)])
        psG = Ring([k.ps(f"F_psG{i}", [128, 1024], F32) for i in range(2)])
        psV = Ring([k.ps(f"F_psV{i}", [128, 512], F32) for i in range(2)])
        psD = Ring([k.ps(f"F_psD{i}", [128, 512], F32) for i in range(2)])
        x1r = Ring([k.sb(f"F_x1{i}", [128, D_MODEL], F32) for i in range(3)])
        x1slots = [k.dma_slot(f"Fx{i}") for i in range(3)]
        oslots = [k.dma_slot(f"Fo{i}") for i in range(3)]
        tiles = []
        for si, (n, L) in enumerate(self.seqs):
            for t0 in range(0, L, 512):
                tiles.append((si, n, t0, min(512, L - t0)))

        def load(ti):
            si, n, t0, nt = tiles[ti]
            L = dict(self.seqs)[n]
            assert nt == 512
            j, hb, rd = hT.next()
            k.wait("sp", *rd)
            col0 = self.off[n] + 2 * si + t0
            w0 = 1 if t0 == 0 else 0
            w1 = nt + 1 if t0 + nt == L else nt + 2
            toks = [hslots[j].dma("sp", hb[:, :, w0:w1], self.h2T_d[:, :, col0 + w0:col0 + w1].rearrange("a p t -> p a t"))]
            if w0 == 1 or w1 == nt + 1:
                k.wait("pool", *rd)
                if w0 == 1:
                    toks.append(k.sig("pool", nc.gpsimd.memset(hb[:, :, 0:1], 0.0)))
                if w1 == nt + 1:
                    toks.append(k.sig("pool", nc.gpsimd.memset(hb[:, :, nt + 1:nt + 2], 0.0)))
            return j, hb, toks

        nxt = load(0)
        for ti in range(len(tiles)):
            j, hb, t_h = nxt
            if ti + 1 < len(tiles):
                nxt = load(ti + 1)
            si, n, t0, nt = tiles[ti]
            tok_base = self.off[n] + t0
            last_pe_h = None
            for fc in range(NFC):
                fs = slice(fc * 128, (fc + 1) * 128)
                gi, pg, rdg = psG.next()
                vi, pv, rdv = psV.next()
                k.wait("pe", *t_h, t_w, *rdg, *rdv)
                for c in range(8):
                    nc.tensor.matmul(pg[:, 0:nt], wg[:, c, fs], hb[:, c, 0:nt], start=(c == 0), stop=(c == 7))
                for c in range(8):
                    nc.tensor.matmul(pg[:, 512:514], wg[:, c, fs], hb[:, c, nt:nt + 2], start=(c == 0), stop=(c == 7))
                for c in range(8):
                    ins = nc.tensor.matmul(pv[:, 0:nt], wv[:, c, fs], hb[:, c, 1:nt + 1], start=(c == 0), stop=(c == 7))
                t_up = k.sig("pe", ins)
                last_pe_h = t_up
                ci, cbb, rd = cbuf.next()
                k.wait("dve", t_up, t_c, *rd)
                t1 = k.sig("dve", nc.vector.tensor_scalar(out=cbb[:, 0:nt], in0=pg[:, 1:nt + 1] if nt == 512 else pg[:, 1:nt + 1],
                                                         scalar1=cw[:, 1, fc:fc + 1], scalar2=cb[:, fc:fc + 1], op0=ALU.mult, op1=ALU.add))
                k.wait("dve", t1)
                t2 = k.sig("dve", nc.vector.scalar_tensor_tensor(out=cbb[:, 0:nt], in0=pg[:, 0:nt], scalar=cw[:, 0, fc:fc + 1], in1=cbb[:, 0:nt],
                                                               op0=ALU.mult, op1=ALU.add))
                k.wait("dve", t2)
                t3 = k.sig("dve", nc.vector.scalar_tensor_tensor(out=cbb[:, 0:nt], in0=pg[:, 2:nt + 2], scalar=cw[:, 2, fc:fc + 1], in1=cbb[:, 0:nt],
                                                               op0=ALU.mult, op1=ALU.add))
                psG.done(gi, t3)
                gli, glbb, rd = glb.next()
                k.wait("act", t3, *rd)
                t_g = k.sig("act", nc.scalar.activation(out=glbb[:, 0:nt], in_=cbb[:, 0:nt], func=AF.Gelu))
                cbuf.done(ci, t_g)
                k.wait("dve", t_g, *yT_readers[fc])
                yT_readers[fc] = []
                t_y = k.sig("dve", nc.vector.tensor_tensor(out=yT[:, fc, 0:nt], in0=pv[:, 0:nt], in1=glbb[:, 0:nt], op=ALU.mult))
                psV.done(vi, t_y)
                glb.done(gli, t_y)
                last_y = t_y
            hT.done(j, last_pe_h)
            for s_ in range(nt // 128):
                xi, xb, rd = x1r.next()
                k.wait("sp", *rd)
                tk0 = tok_base + s_ * 128
                t_x = x1slots[xi].dma("sp", xb[:], self.x1_d[tk0:tk0 + 128, :])
                toks_o = []
                for hf in range(2):
                    di, pd, rd = psD.next()
                    k.wait("pe", last_y, *rd)
                    for fc in range(NFC):
                        ins = nc.tensor.matmul(pd[:, :], yT[:, fc, s_ * 128:(s_ + 1) * 128], wd[:, fc, hf * 512:(hf + 1) * 512],
                                               start=(fc == 0), stop=(fc == NFC - 1))
                    t_d = k.sig("pe", ins)
                    k.wait("dve", t_d, t_x)
                    t_r = k.sig("dve", nc.vector.tensor_tensor(out=xb[:, hf * 512:(hf + 1) * 512], in0=pd[:, :], in1=xb[:, hf * 512:(hf + 1) * 512], op=ALU.add))
                    psD.done(di, t_r)
                    toks_o.append(t_r)
                    last_d = t_d
                k.wait("sp", *toks_o)
                t_o = oslots[xi].dma("sp", dst[tk0:tk0 + 128, :], xb[:])
                x1r.done(xi, t_o)
            for fc in range(NFC):
                yT_readers[fc] = [last_d]


def host_common(prog, inp, depth):
    bf = ml_dtypes.bfloat16
    qkg = np.concatenate([np.repeat(inp["q_norm_w"][:depth, :, None, :], 4, axis=2).reshape(depth, 1024),
                          np.repeat(inp["k_norm_w"][:depth, :, None, :], 2, axis=2).reshape(depth, 512)], axis=1)
    rpb = np.asarray(inp["rpb_b"], np.float32)[:depth].reshape(depth, 4, 15 * 31)
    rpb_ext = np.concatenate([rpb, np.full((depth, 4, 1), NEG, np.float32)], axis=2)
    nBt = prog.nBt
    biasB = np.empty((depth, nBt, 128, 2, 2, 128), np.float32)
    for key, ti in prog.keysB.items():
        idx = bias_tile_B(key, None)
        for h in range(2):
            for g in range(2):
                biasB[:, ti, :, h, g, :] = rpb_ext[:, h * 2 + g][:, idx]
    Lmax = max(L for _, L in prog.seqs)
    return dict(
        norm1_w=np.ascontiguousarray(inp["norm1_w"][:depth]), w_in=np.ascontiguousarray(inp["w_in"][:depth]),
        qk_gain=np.ascontiguousarray(qkg), sink_a=np.ascontiguousarray(inp["sink_a"][:depth]),
        out_norm_w=np.ascontiguousarray(inp["out_norm_w"][:depth]), w_out=np.ascontiguousarray(inp["w_out"][:depth]),
        norm2_w=np.ascontiguousarray(inp["norm2_w"][:depth]), w_gate=np.ascontiguousarray(inp["w_gate"][:depth]),
        w_val=np.ascontiguousarray(inp["w_val"][:depth]), conv_w=np.ascontiguousarray(inp["conv_w"][:depth]),
        conv_b=np.ascontiguousarray(inp["conv_b"][:depth]), w_down=np.ascontiguousarray(inp["w_down"][:depth]),
        rope=rope_table(Lmax), biasA=bias_tiles_A().reshape(3, 128, 2, 256).astype(bf),
        biasC=bias_tiles_C().reshape(17, 128, 2, 256).astype(bf),
        biasB=biasB.reshape(depth, nBt, 128, 2, 256), ident=np.eye(128, dtype=np.float32))


_CACHE = {}


def kernel(**inputs):
    inp = {k_: np.asarray(v) for k_, v in inputs.items()}
    seqs = [("p0", 4096), ("p1", 4096), ("s", 16384)]
    if "prog" not in _CACHE:
        prog = Prog(seqs, depth=2)
        _CACHE["prog"] = (prog, prog.build())
    prog, nc = _CACHE["prog"]
    common = host_common(prog, inp, 2)
    xp, xs = inp["x_prompt"], inp["x_sample"]
    in_maps = []
    for c in range(N_CORES):
        x = np.concatenate([xp[2 * c], xp[2 * c + 1], xs[0]], axis=0)
        in_maps.append(dict(common, x=np.ascontiguousarray(x, dtype=np.float32)))
    res = run_bass_kernel_spmd(nc, in_maps, core_ids=list(range(N_CORES)))
    y_prompt = np.empty((16, 4096, D_MODEL), np.float32)
    for c in range(N_CORES):
        y = res.results[c]["y"]
        y_prompt[2 * c] = y[0:4096]
        y_prompt[2 * c + 1] = y[4096:8192]
    y_sample = np.ascontiguousarray(res.results[0]["y"][8192:]).reshape(1, 16384, D_MODEL).astype(np.float32)
    return (y_prompt, y_sample)
```

```python
import contextlib
import math
import numpy as np
import ml_dtypes
import concourse.bass as bass
import concourse.mybir as mybir
from concourse.bass_utils import run_bass_kernel_spmd

F32 = mybir.dt.float32
BF16 = mybir.dt.bfloat16
AF = mybir.ActivationFunctionType
ALU = mybir.AluOpType
AX = mybir.AxisListType

D_MODEL = 1024
N_MIX = 4
HEAD_DIM = 64
IN_WIDTH = 2048
D_FF = 2816
NFC = D_FF // 128
GRID_W = 64
EPS = 1e-6
NEG = -30000.0
VW = 80
N_CORES = 8
SEM_ROLL = 20000


class K:
    def __init__(self, nc, es):
        self.nc = nc
        self.es = es
        self.eng = {"pe": nc.tensor, "act": nc.scalar, "dve": nc.vector, "pool": nc.gpsimd, "sp": nc.sync}
        self.waited = {e: {} for e in self.eng}
        self.streams = {}
        self.nsem = 0
        self.n_instr = 0
        self.scopes = [es]
        self.slots = {}

    def new_sem(self, name):
        self.nsem += 1
        return self.es.enter_context(self.nc.semaphore(f"{name}_{self.nsem}"))

    def sb(self, name, shape, dt):
        self.uid = getattr(self, "uid", 0) + 1
        return self.scopes[-1].enter_context(self.nc.sbuf_tensor(f"{name}_{self.uid}", list(shape), dt))

    def ps(self, name, shape, dt):
        self.uid = getattr(self, "uid", 0) + 1
        return self.scopes[-1].enter_context(self.nc.psum_tensor(f"{name}_{self.uid}", list(shape), dt))

    @contextlib.contextmanager
    def scope(self):
        st = contextlib.ExitStack()
        self.scopes.append(st)
        try:
            yield
            self.barrier()
        finally:
            self.scopes.pop()
            st.close()

    def all_tokens(self):
        toks = []
        for e, st in self.streams.items():
            toks.append((st[2], st[0], st[1]))
        for sl in self.slots.values():
            if sl.cnt:
                toks.append(sl.tok())
        return toks

    def barrier(self):
        toks = self.all_tokens()
        for e in self.eng:
            self.wait(e, *toks)

    def sig(self, e, instr):
        st = self.streams.get(e)
        if st is None or st[1] >= SEM_ROLL:
            st = [self.new_sem("s" + e), 0, self.nsem]
            self.streams[e] = st
        instr.then_inc(st[0], 1)
        st[1] += 1
        self.n_instr += 1
        return (st[2], st[0], st[1])

    def wait(self, e, *toks):
        w = self.waited[e]
        for t in toks:
            if t is None:
                continue
            key, sem, val = t
            if w.get(key, 0) >= val:
                continue
            self.eng[e].wait_ge(sem, val)
            w[key] = val

    def dma_slot(self, name):
        if name not in self.slots:
            self.slots[name] = DmaSlot(self, name)
        return self.slots[name]


class DmaSlot:
    def __init__(self, k, name):
        self.k = k
        self.sem = k.new_sem("d" + name)
        self.key = k.nsem
        self.cnt = 0

    def dma(self, e, out, in_):
        self.k.eng[e].dma_start(out=out, in_=in_).then_inc(self.sem, 16)
        self.cnt += 16
        self.k.n_instr += 1
        return (self.key, self.sem, self.cnt)

    def tok(self):
        return (self.key, self.sem, self.cnt)


class Ring:
    STRICT = True
    VIOLATIONS = []

    def __init__(self, bufs, name="ring"):
        self.bufs = bufs
        self.n = len(bufs)
        self.k = 0
        self.readers = [[] for _ in bufs]
        self.open = [False] * self.n
        self.name = name

    def next(self):
        i = self.k % self.n
        self.k += 1
        if Ring.STRICT and self.open[i]:
            Ring.VIOLATIONS.append((self.name, i, self.k - 1))
        self.open[i] = True
        r = self.readers[i]
        self.readers[i] = []
        return i, self.bufs[i], r

    def done(self, i, *toks):
        self.open[i] = False
        self.readers[i].extend(t for t in toks if t is not None)


def alibi_slopes_np():
    n = 8
    s = 2.0 ** (-8.0 * np.arange(1, n + 1, dtype=np.float64) / n)
    return s[0::2].reshape(2, 2), s[1::2].reshape(2, 2)


def bias_tiles_A():
    sa, _ = alibi_slopes_np()
    j = np.arange(128)[:, None]
    i = np.arange(128)[None, :]
    out = np.zeros((3, 128, 2, 2, 128), np.float32)
    for r, rel in enumerate((-1, 0, 1)):
        d = rel * 128 + j - i
        for h in range(2):
            for g in range(2):
                out[r, :, h, g, :] = np.where(np.abs(d) <= 128, -sa[h, g] * np.abs(d), NEG)
    return out


def bias_tiles_C():
    _, sc = alibi_slopes_np()
    j = np.arange(128)[:, None]
    i = np.arange(128)[None, :]
    out = np.zeros((17, 128, 2, 2, 128), np.float32)
    for r, rel in enumerate(range(-8, 9)):
        d = rel * 128 + j - i
        ad = np.abs(d)
        mult = (ad <= 64).astype(np.float64) + ((d % 4 == 0) & (ad <= 256)) + ((d % 16 == 0) & (ad <= 1024))
        for h in range(2):
            for g in range(2):
                val = -sc[h, g] * ad + np.log(np.maximum(mult, 1.0))
                out[r, :, h, g, :] = np.where(mult > 0, val, NEG)
    return out


def nbr_plan(L):
    rows = L // GRID_W
    kr = min(8, rows)
    nb = L // 128
    plan = []
    for b in range(nb):
        r0s = [int(np.clip(r - kr // 2, 0, rows - kr)) for r in (2 * b, 2 * b + 1)]
        chunks = sorted({rr // 2 for r0 in r0s for rr in range(r0, r0 + kr)})
        plan.append([(c, (r0s[0] - 2 * b, r0s[1] - (2 * b + 1), c - b)) for c in chunks])
    return plan


def bias_tile_B(key, rpb_idx):
    o0, o1, relc = key
    col = np.arange(GRID_W)
    c0 = np.clip(col - 8, 0, GRID_W - 16)
    idx = np.full((128, 128), 15 * 31, np.int64)
    for qi in range(128):
        qr_rel, qc = divmod(qi, GRID_W)
        r0_rel = (o0, o1 + 1)[qr_rel]
        for kj in range(128):
            kr_rel2, kc = divmod(kj, GRID_W)
            krow_rel = 2 * relc + kr_rel2
            if not (r0_rel <= krow_rel < r0_rel + 8):
                continue
            if not (c0[qc] <= kc < c0[qc] + 16):
                continue
            dr = krow_rel - qr_rel + 7
            dc = kc - qc + 15
            idx[kj, qi] = dr * 31 + dc
    return idx


def rope_table(L):
    t = np.arange(L)
    row = (t // GRID_W).astype(np.float32)
    col = (t % GRID_W).astype(np.float32)
    f = (10000.0 ** (-np.arange(0, 32, 2, dtype=np.float32) / 32)).astype(np.float32)
    ang = np.concatenate([row[:, None] * f[None, :], col[:, None] * f[None, :]], axis=-1).astype(np.float32)
    return np.concatenate([np.cos(ang), np.sin(ang)], axis=-1).astype(np.float32)


class Prog:
    def __init__(self, seqs, depth=2, dbg=None):
        self.seqs = seqs
        self.depth = depth
        self.dbg = dbg or {}
        self.T = sum(L for _, L in seqs)
        self.plan_B()
        self.nBt = len(self.keysB)
        self.off = {}
        o = 0
        for n, L in seqs:
            self.off[n] = o
            o += L

    def build(self):
        nc = bass.Bass("TRN2", target_bir_lowering=False)
        self.nc = nc
        T = self.T
        dep = self.depth
        din = lambda n, s, d=F32: nc.dram_tensor(n, list(s), d, kind="ExternalInput").ap()
        self.x_in = din("x", [T, D_MODEL])
        self.norm1 = din("norm1_w", [dep, D_MODEL])
        self.w_in = din("w_in", [dep, D_MODEL, IN_WIDTH])
        self.qkg = din("qk_gain", [dep, 1536])
        self.sink = din("sink_a", [dep, 4])
        self.onw = din("out_norm_w", [dep, D_MODEL])
        self.w_out = din("w_out", [dep, D_MODEL, D_MODEL])
        self.norm2 = din("norm2_w", [dep, D_MODEL])
        self.w_gate = din("w_gate", [dep, D_MODEL, D_FF])
        self.w_val = din("w_val", [dep, D_MODEL, D_FF])
        self.conv_w = din("conv_w", [dep, 3, D_FF])
        self.conv_b = din("conv_b", [dep, D_FF])
        self.w_down = din("w_down", [dep, D_FF, D_MODEL])
        self.rope = din("rope", [max(L for _, L in self.seqs), 64])
        self.biasA = din("biasA", [3, 128, 2, 256], BF16)
        self.biasC = din("biasC", [17, 128, 2, 256], BF16)
        self.biasB = din("biasB", [dep, self.nBt, 128, 2, 256])
        self.ident_in = din("ident", [128, 128])
        self.y_out = nc.dram_tensor("y", [T, D_MODEL], F32, kind="ExternalOutput").ap()
        dint = lambda n, s, d: nc.dram_tensor(n, list(s), d).ap()
        self.qT_d = dint("qT_d", [8, 128, T], BF16)
        self.kT_d = dint("kT_d", [4, 128, T], BF16)
        self.v_d = dint("v_d", [T, N_MIX * 2 * VW], BF16)
        self.x1_d = dint("x1_d", [T, D_MODEL], F32)
        self.xl_d = dint("xl_d", [T, D_MODEL], F32)
        self.h2T_d = dint("h2T_d", [8, 128, T + 2 * len(self.seqs)], BF16)
        self.o_d = dint("o_d", [T, D_MODEL], BF16)
        if "o" in self.dbg:
            self.dbg_o = nc.dram_tensor("dbg_o", [T, D_MODEL], BF16, kind="ExternalOutput").ap()
        if "qkv" in self.dbg:
            self.dbg_q = nc.dram_tensor("dbg_q", [8, 128, T], BF16, kind="ExternalOutput").ap()
            self.dbg_k = nc.dram_tensor("dbg_k", [4, 128, T], BF16, kind="ExternalOutput").ap()
            self.dbg_v = nc.dram_tensor("dbg_v", [T, N_MIX * 2 * VW], BF16, kind="ExternalOutput").ap()

        with contextlib.ExitStack() as es:
            k = K(nc, es)
            self.k = k
            self.setup_consts()
            final = []
            for l in range(dep):
                x_src = self.x_in if l == 0 else self.xl_d
                with k.scope():
                    self.phase_P(l, x_src)
                if self.dbg.get("stop") == "P":
                    break
                with k.scope():
                    self.phase_M(l)
                if self.dbg.get("stop") == "M":
                    break
                with k.scope():
                    self.phase_O(l, x_src)
                if self.dbg.get("stop") == "O":
                    break
                with k.scope():
                    self.phase_F(l, self.y_out if l == dep - 1 else self.xl_d)
            k.barrier()
        return nc

    def setup_consts(self):
        k, nc = self.k, self.nc
        self.ident_f = k.sb("ident_f", [128, 128], F32)
        self.ident_b = k.sb("ident_b", [128, 128], BF16)
        sl = k.dma_slot("const")
        t = sl.dma("sp", self.ident_f[:], self.ident_in[:, :])
        k.wait("dve", t)
        self.tok_ident = k.sig("dve", nc.vector.tensor_copy(out=self.ident_b[:], in_=self.ident_f[:]))
        self.const_slot = sl
        self.eps_t = k.sb("eps_t", [128, 1], F32)
        self.tok_eps = k.sig("dve", nc.vector.memset(self.eps_t[:], EPS))

    def phase_P(self, l, x_src):
        k, nc = self.k, self.nc
        T = self.T
        NT = T // 128
        sl = k.dma_slot("Pconst")
        w = k.sb(f"P_w{l}", [128, 8, IN_WIDTH], BF16)
        g1 = k.sb(f"P_g1{l}", [128, D_MODEL], F32)
        gqk = k.sb(f"P_gqk{l}", [128, 1536], F32)
        wslot = k.dma_slot("Pw")
        for c in range(8):
            for hh in range(2):
                wslot.dma("pool", w[:, c, hh * 1024:(hh + 1) * 1024], self.w_in[l, c * 128:(c + 1) * 128, hh * 1024:(hh + 1) * 1024])
        t_w = wslot.tok()
        sl.dma("sp", g1[:], self.norm1[l:l + 1, :].broadcast_to([128, D_MODEL]))
        t_g = sl.dma("sp", gqk[:], self.qkg[l:l + 1, :].broadcast_to([128, 1536]))
        k.wait("dve", t_g)
        t_g = k.sig("dve", nc.vector.tensor_scalar(out=gqk[:, 0:1024], in0=gqk[:, 0:1024], scalar1=0.125, scalar2=None, op0=ALU.mult))
        NX = 4
        xr = Ring([k.sb(f"P_x{l}_{i}", [128, D_MODEL], F32) for i in range(NX)])
        xslots = [k.dma_slot(f"Px{i}") for i in range(NX)]
        ropeb = Ring([k.sb(f"P_rope{l}_{i}", [128, 64], F32) for i in range(NX)])
        junk_ln = k.sb(f"P_junk{l}", [128, 1024], F32)
        sqr = Ring([k.sb(f"P_sq{l}_{i}", [128, 1536], F32) for i in range(2)])
        st_ring = Ring([k.sb(f"P_st{l}_{i}", [128, 4], F32) for i in range(2)])
        hb = Ring([k.sb(f"P_hb{l}_{i}", [128, D_MODEL], BF16) for i in range(2)])
        hT = Ring([k.sb(f"P_hT{l}_{i}", [128, 8, 128], BF16) for i in range(2)])
        psT = Ring([k.ps(f"P_psT{l}", [128, 8, 128], BF16)])
        psP = Ring([k.ps(f"P_psP{l}", [128, IN_WIDTH], F32)])
        psQ = Ring([k.ps(f"P_psQ{l}", [128, 12, 128], BF16)])
        s24 = Ring([k.sb(f"P_s24{l}_{i}", [128, 48], F32) for i in range(2)])
        tmp = Ring([k.sb(f"P_tmp{l}_{i}", [128, 1536], F32) for i in range(2)])
        rt = Ring([k.sb(f"P_rt{l}_{i}", [128, 6, 4, 32], F32) for i in range(2)])
        qn = Ring([k.sb(f"P_qn{l}_{i}", [128, 1536], BF16) for i in range(2)])
        NG = 2
        qst = Ring([k.sb(f"P_qst{l}_{i}", [128, 12, 512], BF16) for i in range(NG)])
        vst = Ring([k.sb(f"P_vst{l}_{i}", [128, 4, N_MIX * 2 * VW], BF16) for i in range(NG)])
        oslots = [k.dma_slot(f"Po{i}") for i in range(NG)]
        ones_toks = []
        for b in vst.bufs:
            ones_toks.append(k.sig("pool", nc.gpsimd.memset(b[:], 1.0)))
        seq_of_tile = []
        for n, L in self.seqs:
            for i in range(L // 128):
                seq_of_tile.append(i)

        def load(i):
            j, xb, rd = xr.next()
            k.wait("sp", *rd)
            t1 = xslots[j].dma("sp", xb[:], x_src[i * 128:(i + 1) * 128, :])
            _, rb, rd2 = ropeb.next()
            k.wait("sp", *rd2)
            p = seq_of_tile[i] * 128
            t2 = xslots[j].dma("sp", rb[:], self.rope[p:p + 128, :])
            return (j, xb, rb, t2)

        loaded = {}
        PRE = 2
        for i in range(min(PRE, NT)):
            loaded[i] = load(i)
        out_toks = []
        gi = None
        tails = []
        def front(i):
            if i + PRE < NT:
                loaded[i + PRE] = load(i + PRE)
            xj, xb, rb, t_x = loaded.pop(i)
            si, stt, rd = st_ring.next()
            k.wait("dve", t_x, *rd)
            k.wait("act", self.tok_eps)
            t = k.sig("dve", nc.vector.scalar_tensor_tensor(out=junk_ln[:], in0=xb[:], scalar=1.0, in1=xb[:],
                                                          op0=ALU.mult, op1=ALU.mult, accum_out=stt[:, 0:1]))
            k.wait("act", t)
            t = k.sig("act", nc.scalar.activation(out=stt[:, 1:2], in_=stt[:, 0:1], func=AF.Ln, bias=self.eps_t[:, 0:1], scale=1.0 / D_MODEL))
            k.wait("act", t)
            t = k.sig("act", nc.scalar.activation(out=stt[:, 2:3], in_=stt[:, 1:2], func=AF.Exp, scale=-0.5))
            hi, hbb, rd = hb.next()
            k.wait("dve", t, t_g, *rd)
            t_h = k.sig("dve", nc.vector.scalar_tensor_tensor(out=hbb[:], in0=xb[:], scalar=stt[:, 2:3], in1=g1[:],
                                                            op0=ALU.mult, op1=ALU.mult))
            xr.done(xj, t_h)
            st_ring.done(si, t_h)
            pi, pst, rd = psT.next()
            k.wait("pe", t_h, self.tok_ident, *rd)
            for c in range(8):
                ins = nc.tensor.transpose(pst[:, c, :], hbb[:, c * 128:(c + 1) * 128], self.ident_b[:])
            t_T = k.sig("pe", ins)
            hb.done(hi, t_T)
            ti, hTb, rd = hT.next()
            k.wait("act", t_T, *rd)
            t_hT = k.sig("act", nc.scalar.activation(out=hTb[:], in_=pst[:], func=AF.Copy))
            psT.done(pi, t_hT)
            return xj, rb, ti, hTb, t_hT

        def back(i, fr):
            nonlocal gi, qsb, vsb, grp_toks
            xj, rb, ti, hTb, t_hT = fr
            ppi, pp, rd = psP.next()
            k.wait("pe", t_hT, t_w, *rd)
            for jj in range(4):
                for c in range(8):
                    ins = nc.tensor.matmul(pp[:, jj * 512:(jj + 1) * 512], hTb[:, c, :], w[:, c, jj * 512:(jj + 1) * 512],
                                           start=(c == 0), stop=(c == 7))
            t_P = k.sig("pe", ins)
            hT.done(ti, t_P)
            while tails:
                tails.pop(0)()
            return ppi, pp, t_P

        def back_rest(i, fr, pr):
            nonlocal gi, qsb, vsb, grp_toks
            xj, rb, ti, hTb, t_hT = fr
            ppi, pp, t_P = pr
            sq_i, sqb, rd = sqr.next()
            t_i, tmpb, rd_tmp = tmp.next()
            k.wait("act", t_P, *rd, *rd_tmp)
            t_sq = k.sig("act", nc.scalar.activation(out=sqb[:], in_=pp[:, 0:1536], func=AF.Square))
            t_pc = k.sig("act", nc.scalar.activation(out=tmpb[:], in_=pp[:, 0:1536], func=AF.Copy))
            s_i, s24b, rd = s24.next()
            k.wait("dve", t_sq, *rd)
            t = k.sig("dve", nc.vector.tensor_reduce(out=s24b[:, 0:24], in_=sqb[:].rearrange("p (g d) -> p g d", d=64), axis=AX.X, op=ALU.add))
            sqr.done(sq_i, t)
            k.wait("dve", t)
            k.wait("act", t)
            t = k.sig("act", nc.scalar.activation(out=s24b[:, 24:48], in_=s24b[:, 0:24], func=AF.Ln, bias=self.eps_t[:, 0:1], scale=1.0 / 64))
            k.wait("act", t)
            t = k.sig("act", nc.scalar.activation(out=s24b[:, 0:24], in_=s24b[:, 24:48], func=AF.Exp, scale=-0.5))
            k.wait("dve", t, t_pc)
            t_n1 = k.sig("dve", nc.vector.tensor_tensor(out=tmpb[:].rearrange("p (g d) -> p g d", d=64),
                                                      in0=tmpb[:].rearrange("p (g d) -> p g d", d=64),
                                                      in1=s24b[:, 0:24].unsqueeze(2).broadcast_to([128, 24, 64]), op=ALU.mult))
            s24.done(s_i, t_n1)
            k.wait("dve", t_n1, t_g)
            t_n2 = k.sig("dve", nc.vector.tensor_tensor(out=tmpb[:], in0=tmpb[:], in1=gqk[:], op=ALU.mult))
            q_i, qnb, rd = qn.next()
            k.wait("pool", *rd)
            k.wait("dve", *rd)
            tq = None
            for m in range(3):
                src = tmpb[:, m * 256:(m + 1) * 256].rearrange("p (h g d) -> p g h d", h=2, g=2)
                dst = qnb[:, m * 256:(m + 1) * 256].rearrange("p (g h d) -> p g h d", g=2, h=2)
                k.wait("pool", t_n2)
                tq = k.sig("pool", nc.gpsimd.tensor_copy(out=dst, in_=src))
            tk = k.sig("pool", nc.gpsimd.tensor_copy(out=qnb[:, 1024:1024 + 384], in_=tmpb[:, 1024:1024 + 384]))
            r_i, rtb, rd = rt.next()
            k.wait("dve", t_n2, *rd)
            cosb = rb[:, 0:32].unsqueeze(1).broadcast_to([128, 4, 32])
            sinb = rb[:, 32:64].unsqueeze(1).broadcast_to([128, 4, 32])
            qv = tmpb[:, 768:1024].rearrange("p (a d two) -> p a d two", a=4, two=2)
            kv = tmpb[:, 1408:1536].rearrange("p (a d two) -> p a d two", a=2, two=2)
            qdst = qnb[:, 768:1024].rearrange("p (g h d two) -> p h g d two", g=2, h=2, two=2)
            kdst = qnb[:, 1408:1536].rearrange("p (a d two) -> p a d two", a=2, two=2)
            last = None
            for (src, na, dsts) in ((qv, 4, "q"), (kv, 2, "k")):
                cb = cosb[:, 0:na, :]
                sb_ = sinb[:, 0:na, :]
                xe, xo = src[:, :, :, 0], src[:, :, :, 1]
                b0 = 0 if dsts == "q" else 3
                A = rtb[:, b0 + 0, 0:na, :]
                B = rtb[:, b0 + 1, 0:na, :]
                C = rtb[:, b0 + 2, 0:na, :]
                t1 = k.sig("dve", nc.vector.tensor_tensor(out=A, in0=xe, in1=cb, op=ALU.mult))
                t2 = k.sig("dve", nc.vector.tensor_tensor(out=B, in0=xo, in1=sb_, op=ALU.mult))
                k.wait("dve", t1, t2)
                if dsts == "q":
                    oe = qdst.rearrange("p h g d two -> p (h g) d two")[:, :, :, 0] if False else None
                if dsts == "q":
                    for h in range(2):
                        t3 = k.sig("dve", nc.vector.tensor_tensor(out=qdst[:, h, :, :, 0], in0=A[:, 2 * h:2 * h + 2, :], in1=B[:, 2 * h:2 * h + 2, :], op=ALU.subtract))
                else:
                    t3 = k.sig("dve", nc.vector.tensor_tensor(out=kdst[:, :, :, 0], in0=A, in1=B, op=ALU.subtract))
                k.wait("dve", t3)
                t4 = k.sig("dve", nc.vector.tensor_tensor(out=A, in0=xe, in1=sb_, op=ALU.mult))
                t5 = k.sig("dve", nc.vector.tensor_tensor(out=C, in0=xo, in1=cb, op=ALU.mult))
                k.wait("dve", t4, t5)
                if dsts == "q":
                    for h in range(2):
                        last = k.sig("dve", nc.vector.tensor_tensor(out=qdst[:, h, :, :, 1], in0=A[:, 2 * h:2 * h + 2, :], in1=C[:, 2 * h:2 * h + 2, :], op=ALU.add))
                else:
                    last = k.sig("dve", nc.vector.tensor_tensor(out=kdst[:, :, :, 1], in0=A, in1=C, op=ALU.add))
            t_rope = last
            rt.done(r_i, t_rope)
            ropeb.done(xj, t_rope)
            tmp.done(t_i, t_rope, tk, tq)
            sub = i % 4
            if sub == 0:
                gi, qsb, rdq = qst.next()
                _, vsb, rdv = vst.next()
                k.wait("act", *rdq, *rdv, *ones_toks)
                grp_toks = []
            vdst = vsb[:, sub, :].rearrange("p (a c) -> p a c", c=VW)[:, :, 0:64]
            k.wait("act", t_P)
            t_v = k.sig("act", nc.scalar.activation(out=vdst, in_=pp[:, 1536:2048].rearrange("p (a c) -> p a c", c=64), func=AF.Copy))
            psP.done(ppi, t_v, t_pc, t_sq)

            def tail(i=i, sub=sub, gi=gi, qsb=qsb, vsb=vsb, grp_toks=grp_toks, t_rope=t_rope, tq=tq, tk=tk, q_i=q_i, qnb=qnb, t_v=t_v):
                qi_, pq, rd = psQ.next()
                k.wait("pe", t_rope, tq, tk, *rd)
                for a_ in range(12):
                    ins = nc.tensor.transpose(pq[:, a_, :], qnb[:, a_ * 128:(a_ + 1) * 128], self.ident_b[:])
                t_qT = k.sig("pe", ins)
                qn.done(q_i, t_qT)
                k.wait("act", t_qT)
                t_qs = k.sig("act", nc.scalar.activation(out=qsb[:, :, sub * 128:(sub + 1) * 128], in_=pq[:], func=AF.Copy))
                psQ.done(qi_, t_qs)
                grp_toks += [t_qs, t_v]
                if sub == 3 or i == NT - 1:
                    t0 = (i // 4) * 512
                    n = (sub + 1) * 128
                    k.wait("sp", *grp_toks)
                    os_ = oslots[gi]
                    os_.dma("sp", self.qT_d[:, :, t0:t0 + n].rearrange("a p t -> p a t"), qsb[:, 0:8, 0:n])
                    os_.dma("sp", self.kT_d[:, :, t0:t0 + n].rearrange("a p t -> p a t"), qsb[:, 8:12, 0:n])
                    tt = os_.dma("sp", self.v_d[t0:t0 + n, :].rearrange("(s p) c -> p s c", p=128), vsb[:, 0:sub + 1, :])
                    if "qkv" in self.dbg and l == self.dbg["qkv"]:
                        os_.dma("sp", self.dbg_q[:, :, t0:t0 + n].rearrange("a p t -> p a t"), qsb[:, 0:8, 0:n])
                        os_.dma("sp", self.dbg_k[:, :, t0:t0 + n].rearrange("a p t -> p a t"), qsb[:, 8:12, 0:n])
                        tt = os_.dma("sp", self.dbg_v[t0:t0 + n, :].rearrange("(s p) c -> p s c", p=128), vsb[:, 0:sub + 1, :])
                    qst.done(gi, tt)
                    vst.done(gi, tt)
                    out_toks.append(tt)
            tails.append(tail)

        qsb = vsb = grp_toks = None
        fr = front(0)
        for i in range(NT):
            pr = back(i, fr)
            fr_next = front(i + 1) if i + 1 < NT else None
            back_rest(i, fr, pr)
            fr = fr_next
        for tl in tails:
            tl()
        return out_toks[-NG:]

    def plan_B(self):
        if hasattr(self, "_planB"):
            return
        self._planB = {}
        keys = {}
        for n, L in self.seqs:
            pl = nbr_plan(L)
            self._planB[n] = pl
            for blk in pl:
                for c, key in blk:
                    if key not in keys:
                        keys[key] = len(keys)
        self.keysB = keys

    def chunk_list(self, m, b, name, L, bA, bB, bC, h):
        nb = L // 128
        if m == 3:
            return [(c, None) for c in range(nb)]
        if m == 0:
            return [(c, bA[:, c - b + 1, h, :]) for c in (b - 1, b, b + 1) if 0 <= c < nb]
        if m == 1:
            return [(c, bB[:, self.keysB[key], h, :]) for c, key in self._planB[name][b]]
        return [(c, bC[:, c - b + 8, h, :]) for c in range(b - 8, b + 9) if 0 <= c < nb]

    def phase_M(self, l):
        k, nc = self.k, self.nc
        self.plan_B()
        nBt = len(self.keysB)
        assert nBt == self.nBt, (nBt, self.nBt)
        HALO = (128, 512, 1024, 0)
        bA = k.sb("M_bA", [128, 3, 2, 256], BF16)
        bC = k.sb("M_bC", [128, 17, 2, 256], BF16)
        bB = k.sb("M_bB", [128, nBt, 2, 256], BF16)
        snk = k.sb("M_snk", [128, 4], F32)
        cs = k.dma_slot("Mconst")
        cs.dma("sp", bA[:], self.biasA.rearrange("r p h c -> p r h c"))
        cs.dma("sp", bC[:], self.biasC.rearrange("r p h c -> p r h c"))
        for r in range(nBt):
            cs.dma("pool", bB[:, r, :, :], self.biasB[l, r])
        t_c = cs.dma("sp", snk[:], self.sink[l:l + 1, :].broadcast_to([128, 4]))
        k.wait("act", t_c)
        t_snk = k.sig("act", nc.scalar.activation(out=snk[:], in_=snk[:], func=AF.Exp))
        steps = []
        for n, L in self.seqs:
            QS = L if L <= 4096 else 2048
            for q_lo in range(0, L, QS):
                for m in range(4):
                    if m == 3:
                        lo, hi = 0, L
                    else:
                        lo, hi = max(0, q_lo - HALO[m]), min(L, q_lo + QS + HALO[m])
                    steps.append((n, L, q_lo, QS, m, lo, hi))
        kmax = [0, 0]
        qmax = 0
        for (n, L, q_lo, QS, m, lo, hi) in steps:
            si = 1 if m in (0, 2) else 0
            kmax[si] = max(kmax[si], hi - lo)
            qmax = max(qmax, QS)
        sets = []
        for si in range(2):
            sets.append(dict(kT=k.sb(f"M_kT{si}", [128, kmax[si]], BF16), v=k.sb(f"M_v{si}", [128, kmax[si] // 128, 2 * VW], BF16),
                             q=k.sb(f"M_q{si}", [128, 2, qmax], BF16), slot=k.dma_slot(f"Mset{si}"), readers=[]))
        Sring = Ring([k.ps(f"M_S{i}", [128, 1024], F32) for i in range(3)], "M.Sring")
        Aring = Ring([k.ps(f"M_acc{i}", [128, 512], F32) for i in range(2)], "M.Aring")
        Pring = Ring([k.sb(f"M_P{i}", [128, 1024], BF16) for i in range(3)], "M.Pring")
        Oring = Ring([k.sb(f"M_osb{i}", [80, 2, 512], F32) for i in range(2)], "M.Oring")
        rec_r = Ring([k.sb(f"M_rec{i}", [128, 4], F32) for i in range(2)])
        NOS = 4
        ost = Ring([k.sb(f"M_ost{i}", [128, 256], BF16) for i in range(NOS)])
        oslots = [k.dma_slot(f"Mo{i}") for i in range(NOS)]

        def load(step):
            n, L, q_lo, QS, m, lo, hi = step
            st = sets[1 if m in (0, 2) else 0]
            k.wait("sp", *st["readers"])
            st["readers"] = []
            sl = st["slot"]
            so = self.off[n]
            nk = hi - lo
            sl.dma("sp", st["kT"][:, 0:nk], self.kT_d[m, :, so + lo:so + hi])
            for c0 in range(0, nk // 128, 8):
                c1 = min(nk // 128, c0 + 8)
                sl.dma("sp", st["v"][:, c0:c1, :],
                       self.v_d[so + lo + c0 * 128:so + lo + c1 * 128, m * 2 * VW:(m + 1) * 2 * VW].rearrange("(c p) x -> p c x", p=128))
            for g in range(2):
                sl.dma("sp", st["q"][:, g, 0:QS], self.qT_d[m * 2 + g, :, so + q_lo:so + q_lo + QS])
            return st, sl.tok()

        nxt = load(steps[0])
        for sidx, step in enumerate(steps):
            st, t_ld = nxt
            if sidx + 1 < len(steps):
                nxt = load(steps[sidx + 1])
            n, L, q_lo, QS, m, lo, hi = step
            so = self.off[n]
            NT = 256 if (m == 3 and QS >= 256) else 128
            Nq = 2 * NT
            items = []
            for t0 in range(q_lo, q_lo + QS, NT):
                cls = [self.chunk_list(m, t0 // 128, n, L, bA, bB, bC, h) for h in range(2)]
                assert len(cls[0]) == len(cls[1])
                for ci in range(len(cls[0])):
                    for h in range(2):
                        c, bias = cls[h][ci]
                        items.append((t0, h, c, bias, ci == 0, ci == len(cls[0]) - 1))
            nit = len(items)
            G = 1024 // Nq
            assert nit % 2 == 0

            def colof(pos):
                return (pos % 2) * 512 + (pos // 2) * (Nq if Nq == 256 else 0)
            batches = [list(range(i0, min(nit, i0 + G))) for i0 in range(0, nit, G)]
            nbt = len(batches)
            LAB = 2
            binfo = {}
            pending = []
            cur_acc = {}
            cur_osb = {}
            last_pe = None

            def emit_S(bi):
                ensure_slot_free()
                si_, sps, rd = Sring.next()
                k.wait("pe", t_ld, *rd)
                ins = None
                bt = batches[bi]
                for j0 in range(0, len(bt), 2):
                    for pos in (j0, j0 + 1):
                        t0, h, c, bias, first, last = items[bt[pos]]
                        hp = slice(h * 64, (h + 1) * 64)
                        kc = (c * 128 - lo)
                        cols = slice(colof(pos), colof(pos) + Nq)
                        ins = nc.tensor.matmul(sps[:, cols], st["kT"][hp, kc:kc + 128], st["q"][hp, :, t0 - q_lo:t0 - q_lo + NT],
                                               start=True, stop=(bias is None))
                    for pos in (j0, j0 + 1):
                        bias = items[bt[pos]][3]
                        if bias is not None:
                            k.wait("pe", t_c, self.tok_ident)
                            cols = slice(colof(pos), colof(pos) + Nq)
                            ins = nc.tensor.matmul(sps[:, cols], self.ident_b[:], bias, start=False, stop=True)
                binfo[bi] = (si_, sps, k.sig("pe", ins))

            def epilogue_pe(t0, osb_i, osb, t_ev):
                nonlocal last_pe
                for blk in range(NT // 128):
                    ensure_slot_free()
                    ei, epsb, rd = Sring.next()
                    eps_ = epsb[:, 0:512].rearrange("p (a c) -> p a c", c=128)
                    k.wait("pe", *t_ev, *rd)
                    for h in range(2):
                        for g in range(2):
                            ins = nc.tensor.transpose(eps_[:, h * 2 + g, 0:VW], osb[0:VW, h, g * NT + blk * 128:g * NT + blk * 128 + 128],
                                                      self.ident_f[0:VW, 0:VW])
                    t_tr = k.sig("pe", ins)
                    last_pe = t_tr
                    if blk == NT // 128 - 1:
                        Oring.done(osb_i, t_tr)
                    ri, rec, rd = rec_r.next()
                    k.wait("dve", t_tr, *rd)
                    if m == 0:
                        k.wait("dve", t_snk)
                        t = k.sig("dve", nc.vector.tensor_tensor(out=rec[:], in0=eps_[:, :, 64], in1=snk[:], op=ALU.add))
                        k.wait("dve", t)
                        t = k.sig("dve", nc.vector.reciprocal(out=rec[:], in_=rec[:]))
                    else:
                        t = k.sig("dve", nc.vector.reciprocal(out=rec[:], in_=eps_[:, :, 64]))
                    oi, ob, rd = ost.next()
                    k.wait("dve", t, *rd)
                    t_o = k.sig("dve", nc.vector.tensor_tensor(out=ob[:].rearrange("p (a d) -> p a d", d=64), in0=eps_[:, :, 0:64],
                                                             in1=rec[:].unsqueeze(2).broadcast_to([128, 4, 64]), op=ALU.mult))
                    Sring.done(ei, t_o)
                    rec_r.done(ri, t_o)
                    k.wait("sp", t_o)
                    tok0 = so + t0 + blk * 128
                    t_d = oslots[oi].dma("sp", self.o_d[tok0:tok0 + 128, m * 256:(m + 1) * 256], ob[:])
                    ost.done(oi, t_d)

            def emit_EP(bj):
                nonlocal last_pe
                si_, sps, t_S = binfo.pop(bj)
                pi, pb, rd = Pring.next()
                nb_ = len(batches[bj])
                k.wait("act", t_S, *rd)
                wcol = 512 if Nq == 512 else (nb_ // 2) * 256
                t_E = k.sig("act", nc.scalar.activation(out=pb[:, :].rearrange("p (h c) -> p h c", h=2)[:, :, 0:wcol],
                                                        in_=sps[:, :].rearrange("p (h c) -> p h c", h=2)[:, :, 0:wcol], func=AF.Exp))
                Sring.done(si_, t_E)
                for pos, idx in enumerate(batches[bj]):
                    t0, h, c, bias, first, last = items[idx]
                    if first:
                        ai, acc, rd = Aring.next()
                        cur_acc[h] = (ai, acc)
                        k.wait("pe", *rd)
                    ai, acc = cur_acc[h]
                    k.wait("pe", t_E)
                    cl_ = c - lo // 128
                    t_PV = k.sig("pe", nc.tensor.matmul(acc[0:VW, 0:Nq], st["v"][:, cl_, h * VW:(h + 1) * VW], pb[:, colof(pos):colof(pos) + Nq],
                                                        start=first, stop=last))
                    last_pe = t_PV
                    if last:
                        if h == 0:
                            oi_, osb, rd = Oring.next()
                            cur_osb["o"] = (oi_, osb, [])
                            k.wait("dve", *rd)
                        oi_, osb, evs = cur_osb["o"]
                        k.wait("dve", t_PV)
                        t_ev = k.sig("dve", nc.vector.tensor_copy(out=osb[0:VW, h, 0:Nq], in_=acc[0:VW, 0:Nq]))
                        Aring.done(ai, t_ev)
                        evs.append(t_ev)
                        if h == 1:
                            pending.append((bj + 1, (t0, oi_, osb, list(evs))))
                Pring.done(pi, last_pe)

            ep_state = {"next": 0, "in_epi": False}

            def drain_pending():
                if ep_state["in_epi"]:
                    return
                ep_state["in_epi"] = True
                while pending:
                    _, args = pending.pop(0)
                    epilogue_pe(*args)
                ep_state["in_epi"] = False

            def run_ep_until(b):
                while ep_state["next"] <= b:
                    emit_EP(ep_state["next"])
                    ep_state["next"] += 1
                    drain_pending()

            def ensure_slot_free():
                while Sring.open[Sring.k % Sring.n]:
                    assert ep_state["next"] in binfo, "score ring slot still open but no emitted batch left to consume"
                    emit_EP(ep_state["next"])
                    ep_state["next"] += 1

            for bidx in range(nbt):
                emit_S(bidx)
                drain_pending()
                run_ep_until(bidx - LAB)
            run_ep_until(nbt - 1)
            drain_pending()
            st["readers"] = [last_pe]
        if "o" in self.dbg and l == self.dbg["o"]:
            k.barrier()
            t = k.dma_slot("dbgo").dma("sp", self.dbg_o[:, :], self.o_d[:, :])
            k.wait("sp", t)

    def phase_O(self, l, x_src):
        k, nc = self.k, self.nc
        T = self.T
        NT = T // 128
        cs = k.dma_slot("Oconst")
        w = k.sb("O_w", [128, 8, D_MODEL], BF16)
        gon = k.sb("O_gon", [128, D_MODEL], F32)
        g2 = k.sb("O_g2", [128, D_MODEL], F32)
        wslot = k.dma_slot("Ow")
        for c in range(8):
            wslot.dma("pool", w[:, c, :], self.w_out[l, c * 128:(c + 1) * 128, :])
        t_w = wslot.tok()
        cs.dma("sp", gon[:], self.onw[l:l + 1, :].broadcast_to([128, D_MODEL]))
        t_g = cs.dma("sp", g2[:], self.norm2[l:l + 1, :].broadcast_to([128, D_MODEL]))
        NX = 4
        xr = Ring([k.sb(f"O_x{i}", [128, D_MODEL], F32) for i in range(NX)])
        orr = Ring([k.sb(f"O_o{i}", [128, D_MODEL], BF16) for i in range(NX)])
        xslots = [k.dma_slot(f"Ox{i}") for i in range(NX)]
        sq = Ring([k.sb(f"O_sq{i}", [128, D_MODEL], F32) for i in range(2)])
        stt_r = Ring([k.sb(f"O_st{i}", [128, 16], F32) for i in range(2)])
        tmp = Ring([k.sb(f"O_tmp{i}", [128, D_MODEL], F32) for i in range(2)])
        onb = Ring([k.sb(f"O_on{i}", [128, D_MODEL], BF16) for i in range(2)])
        onT = Ring([k.sb(f"O_onT{i}", [128, 8, 128], BF16) for i in range(2)])
        psT = Ring([k.ps(f"O_psT{i}", [128, 8, 128], BF16) for i in range(2)])
        psY = Ring([k.ps(f"O_psY{i}", [128, D_MODEL], F32) for i in range(2)])
        x1r = Ring([k.sb(f"O_x1{i}", [128, D_MODEL], F32) for i in range(3)])
        x1slots = [k.dma_slot(f"Ox1{i}") for i in range(3)]
        junk = k.sb("O_junk", [128, D_MODEL], F32)
        hb = Ring([k.sb(f"O_hb{i}", [128, D_MODEL], BF16) for i in range(2)])
        NG = 2
        hst = Ring([k.sb(f"O_hst{i}", [128, 8, 512], BF16) for i in range(NG)])
        hslots = [k.dma_slot(f"Oh{i}") for i in range(NG)]
        tile_seq = []
        for si, (n, L) in enumerate(self.seqs):
            for i in range(L // 128):
                tile_seq.append((si, n, i))

        def load(i):
            j, xb, rd = xr.next()
            _, ob, rd2 = orr.next()
            k.wait("sp", *rd, *rd2)
            xslots[j].dma("sp", xb[:], x_src[i * 128:(i + 1) * 128, :])
            t = xslots[j].dma("sp", ob[:], self.o_d[i * 128:(i + 1) * 128, :])
            return j, xb, ob, t

        PRE = 2
        loaded = {i: load(i) for i in range(min(PRE, NT))}
        grp = None
        tails = []
        def front(i):
            if i + PRE < NT:
                loaded[i + PRE] = load(i + PRE)
            xj, xb, ob, t_x = loaded.pop(i)
            sqi, sqb, rd = sq.next()
            k.wait("act", t_x, *rd)
            t_sq = k.sig("act", nc.scalar.activation(out=sqb[:], in_=ob[:], func=AF.Square))
            si_, stt, rd = stt_r.next()
            k.wait("dve", t_sq, *rd)
            t = k.sig("dve", nc.vector.tensor_reduce(out=stt[:, 0:4], in_=sqb[:].rearrange("p (m d) -> p m d", d=256), axis=AX.X, op=ALU.add))
            sq.done(sqi, t)
            k.wait("act", t, self.tok_eps)
            t = k.sig("act", nc.scalar.activation(out=stt[:, 4:8], in_=stt[:, 0:4], func=AF.Ln, bias=self.eps_t[:, 0:1], scale=1.0 / 256))
            k.wait("act", t)
            t = k.sig("act", nc.scalar.activation(out=stt[:, 8:12], in_=stt[:, 4:8], func=AF.Exp, scale=-0.5))
            ti, tmpb, rd = tmp.next()
            k.wait("dve", t, *rd)
            t_n1 = k.sig("dve", nc.vector.tensor_tensor(out=tmpb[:].rearrange("p (m d) -> p m d", d=256), in0=ob[:].rearrange("p (m d) -> p m d", d=256),
                                                      in1=stt[:, 8:12].unsqueeze(2).broadcast_to([128, 4, 256]), op=ALU.mult))
            oi_, onbb, rd = onb.next()
            k.wait("dve", t_n1, t_g, *rd)
            t_n2 = k.sig("dve", nc.vector.tensor_tensor(out=onbb[:], in0=tmpb[:], in1=gon[:], op=ALU.mult))
            tmp.done(ti, t_n2)
            pi, pst, rd = psT.next()
            k.wait("pe", t_n2, self.tok_ident, *rd)
            for c in range(8):
                ins = nc.tensor.transpose(pst[:, c, :], onbb[:, c * 128:(c + 1) * 128], self.ident_b[:])
            t_T = k.sig("pe", ins)
            onb.done(oi_, t_T)
            tti, onTb, rd = onT.next()
            k.wait("act", t_T, *rd)
            t_c = k.sig("act", nc.scalar.activation(out=onTb[:], in_=pst[:], func=AF.Copy))
            psT.done(pi, t_c)
            return xj, xb, tti, onTb, t_c, si_, stt, t_n1, t_sq

        def back(i, fr):
            nonlocal grp
            xj, xb, tti, onTb, t_c, si_, stt, t_n1, t_sq = fr
            yi, py, rd = psY.next()
            k.wait("pe", t_c, t_w, *rd)
            for hf in range(2):
                for c in range(8):
                    ins = nc.tensor.matmul(py[:, hf * 512:(hf + 1) * 512], onTb[:, c, :], w[:, c, hf * 512:(hf + 1) * 512], start=(c == 0), stop=(c == 7))
            t_Y = k.sig("pe", ins)
            onT.done(tti, t_Y)
            while tails:
                tails.pop(0)()
            return yi, py, t_Y

        def back_rest(i, fr, pr):
            nonlocal grp
            xj, xb, tti, onTb, t_c, si_, stt, t_n1, t_sq = fr
            yi, py, t_Y = pr
            x1i, x1b, rd = x1r.next()
            k.wait("dve", t_Y, *rd)
            t_x1 = k.sig("dve", nc.vector.tensor_tensor(out=x1b[:], in0=py[:], in1=xb[:], op=ALU.add))
            psY.done(yi, t_x1)
            k.wait("sp", t_x1)
            t_st = x1slots[x1i].dma("sp", self.x1_d[i * 128:(i + 1) * 128, :], x1b[:])
            k.wait("dve", t_x1)
            t = k.sig("dve", nc.vector.scalar_tensor_tensor(out=junk[:], in0=x1b[:], scalar=1.0, in1=x1b[:], op0=ALU.mult, op1=ALU.mult,
                                                          accum_out=stt[:, 12:13]))
            k.wait("act", t)
            t = k.sig("act", nc.scalar.activation(out=stt[:, 13:14], in_=stt[:, 12:13], func=AF.Ln, bias=self.eps_t[:, 0:1], scale=1.0 / D_MODEL))
            k.wait("act", t)
            t = k.sig("act", nc.scalar.activation(out=stt[:, 14:15], in_=stt[:, 13:14], func=AF.Exp, scale=-0.5))
            hi, hbb, rd = hb.next()
            k.wait("dve", t, t_g, *rd)
            t_h = k.sig("dve", nc.vector.scalar_tensor_tensor(out=hbb[:], in0=x1b[:], scalar=stt[:, 14:15], in1=g2[:], op0=ALU.mult, op1=ALU.mult))
            stt_r.done(si_, t_h)
            x1r.done(x1i, t_h, t_st)
            xr.done(xj, t_x1)
            orr.done(xj, t_n1, t_sq)
            sqn, n, ti_in_seq = tile_seq[i]
            sub = ti_in_seq % 4
            if sub == 0:
                gi, hsb, rd = hst.next()
                k.wait("act", *rd)
                grp = (gi, hsb, i)
            gi, hsb, i0 = grp

            def tail(t_h=t_h, hbb=hbb, hi=hi, sqn=sqn, n=n, ti_in_seq=ti_in_seq, sub=sub, gi=gi, hsb=hsb):
                pi, pst, rd = psT.next()
                k.wait("pe", t_h, *rd)
                for c in range(8):
                    ins = nc.tensor.transpose(pst[:, c, :], hbb[:, c * 128:(c + 1) * 128], self.ident_b[:])
                t_T2 = k.sig("pe", ins)
                hb.done(hi, t_T2)
                k.wait("act", t_T2)
                t_hs = k.sig("act", nc.scalar.activation(out=hsb[:, :, sub * 128:(sub + 1) * 128], in_=pst[:], func=AF.Copy))
                psT.done(pi, t_hs)
                L = dict(self.seqs)[n]
                if sub == 3 or ti_in_seq == L // 128 - 1:
                    ncol = (sub + 1) * 128
                    col0 = self.off[n] + 2 * sqn + 1 + (ti_in_seq - sub) * 128
                    k.wait("sp", t_hs)
                    t_d = hslots[gi].dma("sp", self.h2T_d[:, :, col0:col0 + ncol].rearrange("a p t -> p a t"), hsb[:, :, 0:ncol])
                    hst.done(gi, t_d)
            tails.append(tail)

        fr = front(0)
        for i in range(NT):
            pr = back(i, fr)
            fr_next = front(i + 1) if i + 1 < NT else None
            back_rest(i, fr, pr)
            fr = fr_next
        for tl in tails:
            tl()

    def phase_F(self, l, dst):
        k, nc = self.k, self.nc
        wg = k.sb("F_wg", [128, 8, D_FF], BF16)
        wv = k.sb("F_wv", [128, 8, D_FF], BF16)
        wd = k.sb("F_wd", [128, NFC, D_MODEL], BF16)
        cw = k.sb("F_cw", [128, 3, NFC], F32)
        cb = k.sb("F_cb", [128, NFC], F32)
        wslot = k.dma_slot("Fw")
        for c in range(8):
            for (dstw, src) in ((wg, self.w_gate), (wv, self.w_val)):
                for f0 in range(0, D_FF, 1408):
                    wslot.dma("pool", dstw[:, c, f0:f0 + 1408], src[l, c * 128:(c + 1) * 128, f0:f0 + 1408])
        for fc in range(NFC):
            wslot.dma("pool", wd[:, fc, :], self.w_down[l, fc * 128:(fc + 1) * 128, :])
        t_w = wslot.tok()
        cs = k.dma_slot("Fconst")
        with nc.allow_non_contiguous_dma(reason="tiny per-feature conv params"):
            for j in range(3):
                cs.dma("sp", cw[:, j, :], self.conv_w[l, j, :].rearrange("(c p) -> p c", p=128))
            t_c = cs.dma("sp", cb[:], self.conv_b[l, :].rearrange("(c p) -> p c", p=128))
        hT = Ring([k.sb(f"F_hT{i}", [128, 8, 514], BF16) for i in range(2)])
        hslots = [k.dma_slot(f"Fh{i}") for i in range(2)]
        yT = k.sb("F_yT", [128, NFC, 512], BF16)
        yT_readers = [[] for _ in range(NFC)]
        cbuf = Ring([k.sb(f"F_c{i}", [128, 512], F32) for i in range(2)])
        glb = Ring([k.sb(f"F_g{i}", [128, 512], F32) for i in range(2)])
        psG = Ring([k.ps(f"F_psG{i}", [128, 1024], F32) for i in range(2)])
        psV = Ring([k.ps(f"F_psV{i}", [128, 512], F32) for i in range(2)])
        psD = Ring([k.ps(f"F_psD{i}", [128, 512], F32) for i in range(2)])
        x1r = Ring([k.sb(f"F_x1{i}", [128, D_MODEL], F32) for i in range(3)])
        x1slots = [k.dma_slot(f"Fx{i}") for i in range(3)]
        oslots = [k.dma_slot(f"Fo{i}") for i in range(3)]
        tiles = []
        for si, (n, L) in enumerate(self.seqs):
            for t0 in range(0, L, 512):
                tiles.append((si, n, t0, min(512, L - t0)))

        def load(ti):
            si, n, t0, nt = tiles[ti]
            L = dict(self.seqs)[n]
            assert nt == 512
            j, hb, rd = hT.next()
            k.wait("sp", *rd)
            col0 = self.off[n] + 2 * si + t0
            w0 = 1 if t0 == 0 else 0
            w1 = nt + 1 if t0 + nt == L else nt + 2
            toks = [hslots[j].dma("sp", hb[:, :, w0:w1], self.h2T_d[:, :, col0 + w0:col0 + w1].rearrange("a p t -> p a t"))]
            if w0 == 1 or w1 == nt + 1:
                k.wait("pool", *rd)
                if w0 == 1:
                    toks.append(k.sig("pool", nc.gpsimd.memset(hb[:, :, 0:1], 0.0)))
                if w1 == nt + 1:
                    toks.append(k.sig("pool", nc.gpsimd.memset(hb[:, :, nt + 1:nt + 2], 0.0)))
            return j, hb, toks

        nxt = load(0)
        for ti in range(len(tiles)):
            j, hb, t_h = nxt
            if ti + 1 < len(tiles):
                nxt = load(ti + 1)
            si, n, t0, nt = tiles[ti]
            tok_base = self.off[n] + t0
            last_pe_h = None
            for fc in range(NFC):
                fs = slice(fc * 128, (fc + 1) * 128)
                gi, pg, rdg = psG.next()
                vi, pv, rdv = psV.next()
                k.wait("pe", *t_h, t_w, *rdg, *rdv)
                for c in range(8):
                    nc.tensor.matmul(pg[:, 0:nt], wg[:, c, fs], hb[:, c, 0:nt], start=(c == 0), stop=(c == 7))
                for c in range(8):
                    nc.tensor.matmul(pg[:, 512:514], wg[:, c, fs], hb[:, c, nt:nt + 2], start=(c == 0), stop=(c == 7))
                for c in range(8):
                    ins = nc.tensor.matmul(pv[:, 0:nt], wv[:, c, fs], hb[:, c, 1:nt + 1], start=(c == 0), stop=(c == 7))
                t_up = k.sig("pe", ins)
                last_pe_h = t_up
                ci, cbb, rd = cbuf.next()
                k.wait("dve", t_up, t_c, *rd)
                t1 = k.sig("dve", nc.vector.tensor_scalar(out=cbb[:, 0:nt], in0=pg[:, 1:nt + 1] if nt == 512 else pg[:, 1:nt + 1],
                                                         scalar1=cw[:, 1, fc:fc + 1], scalar2=cb[:, fc:fc + 1], op0=ALU.mult, op1=ALU.add))
                k.wait("dve", t1)
                t2 = k.sig("dve", nc.vector.scalar_tensor_tensor(out=cbb[:, 0:nt], in0=pg[:, 0:nt], scalar=cw[:, 0, fc:fc + 1], in1=cbb[:, 0:nt],
                                                               op0=ALU.mult, op1=ALU.add))
                k.wait("dve", t2)
                t3 = k.sig("dve", nc.vector.scalar_tensor_tensor(out=cbb[:, 0:nt], in0=pg[:, 2:nt + 2], scalar=cw[:, 2, fc:fc + 1], in1=cbb[:, 0:nt],
                                                               op0=ALU.mult, op1=ALU.add))
                psG.done(gi, t3)
                gli, glbb, rd = glb.next()
                k.wait("act", t3, *rd)
                t_g = k.sig("act", nc.scalar.activation(out=glbb[:, 0:nt], in_=cbb[:, 0:nt], func=AF.Gelu))
                cbuf.done(ci, t_g)
                k.wait("dve", t_g, *yT_readers[fc])
                yT_readers[fc] = []
                t_y = k.sig("dve", nc.vector.tensor_tensor(out=yT[:, fc, 0:nt], in0=pv[:, 0:nt], in1=glbb[:, 0:nt], op=ALU.mult))
                psV.done(vi, t_y)
                glb.done(gli, t_y)
                last_y = t_y
            hT.done(j, last_pe_h)
            for s_ in range(nt // 128):
                xi, xb, rd = x1r.next()
                k.wait("sp", *rd)
                tk0 = tok_base + s_ * 128
                t_x = x1slots[xi].dma("sp", xb[:], self.x1_d[tk0:tk0 + 128, :])
                toks_o = []
                for hf in range(2):
                    di, pd, rd = psD.next()
                    k.wait("pe", last_y, *rd)
                    for fc in range(NFC):
                        ins = nc.tensor.matmul(pd[:, :], yT[:, fc, s_ * 128:(s_ + 1) * 128], wd[:, fc, hf * 512:(hf + 1) * 512],
                                               start=(fc == 0), stop=(fc == NFC - 1))
                    t_d = k.sig("pe", ins)
                    k.wait("dve", t_d, t_x)
                    t_r = k.sig("dve", nc.vector.tensor_tensor(out=xb[:, hf * 512:(hf + 1) * 512], in0=pd[:, :], in1=xb[:, hf * 512:(hf + 1) * 512], op=ALU.add))
                    psD.done(di, t_r)
                    toks_o.append(t_r)
                    last_d = t_d
                k.wait("sp", *toks_o)
                t_o = oslots[xi].dma("sp", dst[tk0:tk0 + 128, :], xb[:])
                x1r.done(xi, t_o)
            for fc in range(NFC):
                yT_readers[fc] = [last_d]


def host_common(prog, inp, depth):
    bf = ml_dtypes.bfloat16
    qkg = np.concatenate([np.repeat(inp["q_norm_w"][:depth, :, None, :], 4, axis=2).reshape(depth, 1024),
                          np.repeat(inp["k_norm_w"][:depth, :, None, :], 2, axis=2).reshape(depth, 512)], axis=1)
    rpb = np.asarray(inp["rpb_b"], np.float32)[:depth].reshape(depth, 4, 15 * 31)
    rpb_ext = np.concatenate([rpb, np.full((depth, 4, 1), NEG, np.float32)], axis=2)
    nBt = prog.nBt
    biasB = np.empty((depth, nBt, 128, 2, 2, 128), np.float32)
    for key, ti in prog.keysB.items():
        idx = bias_tile_B(key, None)
        for h in range(2):
            for g in range(2):
                biasB[:, ti, :, h, g, :] = rpb_ext[:, h * 2 + g][:, idx]
    Lmax = max(L for _, L in prog.seqs)
    return dict(
        norm1_w=np.ascontiguousarray(inp["norm1_w"][:depth]), w_in=np.ascontiguousarray(inp["w_in"][:depth]),
        qk_gain=np.ascontiguousarray(qkg), sink_a=np.ascontiguousarray(inp["sink_a"][:depth]),
        out_norm_w=np.ascontiguousarray(inp["out_norm_w"][:depth]), w_out=np.ascontiguousarray(inp["w_out"][:depth]),
        norm2_w=np.ascontiguousarray(inp["norm2_w"][:depth]), w_gate=np.ascontiguousarray(inp["w_gate"][:depth]),
        w_val=np.ascontiguousarray(inp["w_val"][:depth]), conv_w=np.ascontiguousarray(inp["conv_w"][:depth]),
        conv_b=np.ascontiguousarray(inp["conv_b"][:depth]), w_down=np.ascontiguousarray(inp["w_down"][:depth]),
        rope=rope_table(Lmax), biasA=bias_tiles_A().reshape(3, 128, 2, 256).astype(bf),
        biasC=bias_tiles_C().reshape(17, 128, 2, 256).astype(bf),
        biasB=biasB.reshape(depth, nBt, 128, 2, 256), ident=np.eye(128, dtype=np.float32))


_CACHE = {}


def kernel(**inputs):
    inp = {k_: np.asarray(v) for k_, v in inputs.items()}
    seqs = [("p0", 4096), ("p1", 4096), ("s", 16384)]
    if "prog" not in _CACHE:
        prog = Prog(seqs, depth=2)
        _CACHE["prog"] = (prog, prog.build())
    prog, nc = _CACHE["prog"]
    common = host_common(prog, inp, 2)
    xp, xs = inp["x_prompt"], inp["x_sample"]
    in_maps = []
    for c in range(N_CORES):
        x = np.concatenate([xp[2 * c], xp[2 * c + 1], xs[0]], axis=0)
        in_maps.append(dict(common, x=np.ascontiguousarray(x, dtype=np.float32)))
    res = run_bass_kernel_spmd(nc, in_maps, core_ids=list(range(N_CORES)))
    y_prompt = np.empty((16, 4096, D_MODEL), np.float32)
    for c in range(N_CORES):
        y = res.results[c]["y"]
        y_prompt[2 * c] = y[0:4096]
        y_prompt[2 * c + 1] = y[4096:8192]
    y_sample = np.ascontiguousarray(res.results[0]["y"][8192:]).reshape(1, 16384, D_MODEL).astype(np.float32)
    return (y_prompt, y_sample)
```
